# Optimizing a Trainium2 kernel written in Bass

```python
import math
import jax
import jax.numpy as jnp
from jax import lax
import numpy as np

D_MODEL = 1024
BATCH = 8
SEQ = 2048
DEPTH = 4

GRID_W = 64
CTX_LEN = 256
EPS = 1e-6
N_MOD = 9

POOL_WIDTH = 256
POOL_WINDOWS = (2, 4, 8, 16)
POOL_GROUP = POOL_WIDTH // len(POOL_WINDOWS)

HYENA_WIDTH = 256
HYENA_ORDER = 2
HYENA_EMB_DIM = 33
HYENA_FILTER_ORDER = 64
HYENA_DECAY_TARGET = 1e-2
HYENA_FAST_DECAY_PCT = 0.3
HYENA_SLOW_DECAY_PCT = 1.5

HEAD_DIM = 64
N_Q_HEADS = 8
N_KV_HEADS = 2
GQA_GROUP = N_Q_HEADS // N_KV_HEADS
ATTN_WIDTH = N_Q_HEADS * HEAD_DIM
KV_WIDTH = N_KV_HEADS * HEAD_DIM
AXIS_DIM = HEAD_DIM // 2
ROPE_THETA = 10000.0
Q_BLOCK = 128
ATTN_SCALE = HEAD_DIM ** -0.5

HYENA_OFF = POOL_WIDTH
Q_OFF = HYENA_OFF + (HYENA_ORDER + 1) * HYENA_WIDTH
K_OFF = Q_OFF + ATTN_WIDTH
V_OFF = K_OFF + KV_WIDTH
IN_WIDTH = V_OFF + KV_WIDTH
MIX_WIDTH = POOL_WIDTH + HYENA_WIDTH + ATTN_WIDTH

D_FF = 2816

kernel_name = 'hymba_pool_hyena_gqa_prefix_dit_block'


def rms_norm(x, g):
    xf = x.astype(jnp.float32)
    y = xf * lax.rsqrt(jnp.mean(xf * xf, axis=-1, keepdims=True) + EPS)
    return (y * g.astype(jnp.float32)).astype(x.dtype)


def modulate(h, shift, scale):
    return h * (1 + scale) + shift


def swiglu(h, w_gate, w_up, w_down):
    return (jax.nn.silu(h @ w_gate) * (h @ w_up)) @ w_down


def ffn_half_step(h, g, shift, scale, gate, w_gate, w_up, w_down):
    y = modulate(rms_norm(h, g), shift, scale)
    return h + 0.5 * gate * swiglu(y, w_gate, w_up, w_down)


def multiscale_pool(u, w_pool, pool_scale):
    B, L, _ = u.shape
    uf = u.astype(jnp.float32)
    cs = jnp.concatenate([jnp.zeros((B, 1, POOL_WIDTH), jnp.float32), jnp.cumsum(uf, axis=1)], axis=1)
    t = jnp.arange(L)
    groups = []
    for gi, win in enumerate(POOL_WINDOWS):
        lo = jnp.clip(t - win // 2, 0, L)
        hi = jnp.clip(t - win // 2 + win, 0, L)
        sl = slice(gi * POOL_GROUP, (gi + 1) * POOL_GROUP)
        seg = cs[..., sl]
        window_sum = jnp.take(seg, hi, axis=1) - jnp.take(seg, lo, axis=1)
        count = (hi - lo).astype(jnp.float32)[None, :, None]
        groups.append(window_sum / count - uf[..., sl])
    pooled = jnp.stack(groups, axis=2).astype(u.dtype)
    mixed = jnp.einsum('blgc,gcd->blgd', pooled, w_pool).reshape(B, L, POOL_WIDTH)
    return mixed * pool_scale


def short_conv3(u, w, b):
    up = jnp.pad(u, ((0, 0), (1, 1), (0, 0)))
    return up[:, :-2] * w[0] + up[:, 1:-1] * w[1] + up[:, 2:] * w[2] + b


def hyena_filters(L, f_w1, f_b1, f_w2, f_b2, f_w3, sin_freq):
    f32 = jnp.float32
    t = jnp.linspace(0.0, 1.0, L, dtype=f32)[:, None]
    bands = (HYENA_EMB_DIM - 1) // 2
    w_ang = 2.0 * math.pi * jnp.arange(L, dtype=f32) / L
    freqs = jnp.linspace(1e-4, bands - 1, bands, dtype=f32)
    ang = w_ang[:, None] * freqs[None, :]
    z = jnp.concatenate([t, jnp.cos(ang), -jnp.sin(ang)], axis=-1)
    sf = sin_freq.astype(f32)
    hdn = jnp.sin(sf[0] * (z @ f_w1.astype(f32) + f_b1.astype(f32)))
    hdn = jnp.sin(sf[1] * (hdn @ f_w2.astype(f32) + f_b2.astype(f32)))
    filt = (hdn @ f_w3.astype(f32)).reshape(L, HYENA_ORDER, 2, HYENA_WIDTH)
    max_decay = math.log(HYENA_DECAY_TARGET) / HYENA_FAST_DECAY_PCT
    min_decay = math.log(HYENA_DECAY_TARGET) / HYENA_SLOW_DECAY_PCT
    deltas = jnp.linspace(min_decay, max_decay, HYENA_WIDTH, dtype=f32)
    decay = jnp.exp(-t * jnp.abs(deltas))
    filt = filt * decay[:, None, None, :]
    fwd = filt[:, :, 0]
    bwd = filt[1:, :, 1][::-1]
    k = jnp.concatenate([fwd, jnp.zeros_like(fwd[:1]), bwd], axis=0)
    return k / jnp.sum(jnp.abs(k), axis=0, keepdims=True)


def fft_long_conv(u, k_f):
    L = u.shape[1]
    u_f = jnp.fft.rfft(u, n=2 * L, axis=1)
    return jnp.fft.irfft(u_f * k_f, n=2 * L, axis=1)[:, :L]


def hyena_mixer(proj, conv_w, conv_b, f_w1, f_b1, f_w2, f_b2, f_w3, sin_freq, hy_bias):
    L = proj.shape[1]
    zc = short_conv3(proj, conv_w, conv_b).astype(jnp.float32)
    v, x1, x2 = jnp.split(zc, 3, axis=-1)
    k_f = jnp.fft.rfft(hyena_filters(L, f_w1, f_b1, f_w2, f_b2, f_w3, sin_freq), axis=0)
    hb = hy_bias.astype(jnp.float32)
    y = v
    for o, gate in enumerate((x1, x2)):
        y = gate * (fft_long_conv(y, k_f[:, o]) + hb[o] * y)
    return y.astype(proj.dtype)


def axial_rope_tables(rows):
    f32 = jnp.float32
    row = jnp.repeat(jnp.arange(rows), GRID_W).astype(f32)
    col = jnp.tile(jnp.arange(GRID_W), rows).astype(f32)
    inv_freq = 1.0 / (ROPE_THETA ** (jnp.arange(0, AXIS_DIM, 2, dtype=f32) / AXIS_DIM))
    ang_r = row[:, None] * inv_freq
    ang_c = col[:, None] * inv_freq
    ang = jnp.concatenate([ang_r, ang_r, ang_c, ang_c], axis=-1)
    return jnp.cos(ang), jnp.sin(ang)


def apply_axial_rope(x, cos, sin):
    xf = x.astype(jnp.float32)
    x1, x2, x3, x4 = jnp.split(xf, 4, axis=-1)
    rot = jnp.concatenate([-x2, x1, -x4, x3], axis=-1)
    return (xf * cos[:, None, :] + rot * sin[:, None, :]).astype(x.dtype)


def attn_heads(p, n_heads, g, cos, sin):
    B, S, _ = p.shape
    hd = rms_norm(p.reshape(B, S, n_heads, HEAD_DIM), g)
    if cos is not None:
        hd = apply_axial_rope(hd, cos, sin)
    return hd


def attend(q, k, v):
    s = jnp.einsum('bqkgd,bskd->bkgqs', q, k, preferred_element_type=jnp.float32) * ATTN_SCALE
    p = jax.nn.softmax(s, axis=-1).astype(v.dtype)
    o = jnp.einsum('bkgqs,bskd->bqkgd', p, v)
    return o.reshape(o.shape[0], o.shape[1], ATTN_WIDTH)


def blocked_attention(q, k, v):
    B, L = q.shape[:2]
    qb = jnp.moveaxis(q.reshape(B, L // Q_BLOCK, Q_BLOCK, N_KV_HEADS, GQA_GROUP, HEAD_DIM), 1, 0)
    out = lax.map(lambda qblk: attend(qblk, k, v), qb)
    return jnp.moveaxis(out, 0, 1).reshape(B, L, ATTN_WIDTH)


def setup_inputs(seed: int = 0) -> dict:
    key = jax.random.key(seed)
    ks = jax.random.split(key, 32)
    f32 = jnp.float32

    def nrm(k, shape, scale):
        return jax.random.normal(k, shape, f32) * scale

    return {
        'x': nrm(ks[0], (BATCH, SEQ, D_MODEL), 1.0),
        'c': nrm(ks[1], (BATCH, D_MODEL), 1.0),
        'ctx': nrm(ks[2], (BATCH, CTX_LEN, D_MODEL), 1.0),
        'c_ctx': nrm(ks[3], (D_MODEL,), 1.0),
        'norm_g': 1.0 + nrm(ks[4], (DEPTH, 3, D_MODEL), 0.05),
        'w_mod': nrm(ks[5], (DEPTH, D_MODEL, N_MOD * D_MODEL), D_MODEL ** -0.5),
        'b_mod': nrm(ks[6], (DEPTH, N_MOD * D_MODEL), 0.01),
        'ffn_w_gate': nrm(ks[7], (DEPTH, 2, D_MODEL, D_FF), D_MODEL ** -0.5),
        'ffn_w_up': nrm(ks[8], (DEPTH, 2, D_MODEL, D_FF), D_MODEL ** -0.5),
        'ffn_w_down': nrm(ks[9], (DEPTH, 2, D_FF, D_MODEL), D_FF ** -0.5),
        'w_in': nrm(ks[10], (DEPTH, D_MODEL, IN_WIDTH), D_MODEL ** -0.5),
        'w_out': nrm(ks[11], (DEPTH, MIX_WIDTH, D_MODEL), MIX_WIDTH ** -0.5),
        'pool_w': nrm(ks[12], (DEPTH, len(POOL_WINDOWS), POOL_GROUP, POOL_GROUP), POOL_GROUP ** -0.5),
        'pool_scale': 1.0 + nrm(ks[13], (DEPTH, POOL_WIDTH), 0.1),
        'hyena_conv_w': nrm(ks[14], (DEPTH, 3, (HYENA_ORDER + 1) * HYENA_WIDTH), 3.0 ** -0.5),
        'hyena_conv_b': nrm(ks[15], (DEPTH, (HYENA_ORDER + 1) * HYENA_WIDTH), 0.01),
        'hyena_f_w1': nrm(ks[16], (DEPTH, HYENA_EMB_DIM, HYENA_FILTER_ORDER), HYENA_EMB_DIM ** -0.5),
        'hyena_f_b1': nrm(ks[17], (DEPTH, HYENA_FILTER_ORDER), 0.1),
        'hyena_f_w2': nrm(ks[18], (DEPTH, HYENA_FILTER_ORDER, HYENA_FILTER_ORDER), HYENA_FILTER_ORDER ** -0.5),
        'hyena_f_b2': nrm(ks[19], (DEPTH, HYENA_FILTER_ORDER), 0.1),
        'hyena_f_w3': nrm(ks[20], (DEPTH, HYENA_FILTER_ORDER, HYENA_ORDER * 2 * HYENA_WIDTH), HYENA_FILTER_ORDER ** -0.5),
        'hyena_sin_freq': 1.0 + nrm(ks[21], (DEPTH, 2, HYENA_FILTER_ORDER), 0.1),
        'hyena_bias': nrm(ks[22], (DEPTH, HYENA_ORDER, HYENA_WIDTH), 1.0),
        'q_norm_g': 1.0 + nrm(ks[23], (DEPTH, HEAD_DIM), 0.05),
        'k_norm_g': 1.0 + nrm(ks[24], (DEPTH, HEAD_DIM), 0.05),
    }


def reference(x, c, ctx, c_ctx, norm_g, w_mod, b_mod, ffn_w_gate, ffn_w_up, ffn_w_down,
              w_in, w_out, pool_w, pool_scale, hyena_conv_w, hyena_conv_b,
              hyena_f_w1, hyena_f_b1, hyena_f_w2, hyena_f_b2, hyena_f_w3,
              hyena_sin_freq, hyena_bias, q_norm_g, k_norm_g):
    B, L, D = x.shape
    C = ctx.shape[1]
    rows = L // GRID_W
    cos, sin = axial_rope_tables(rows)
    c_act = jax.nn.silu(c)
    cc_act = jax.nn.silu(c_ctx)
    h, hc = x, ctx
    for l in range(DEPTH):
        last = l == DEPTH - 1
        mx = (c_act @ w_mod[l] + b_mod[l]).reshape(B, N_MOD, 1, D)
        mc = (cc_act @ w_mod[l] + b_mod[l]).reshape(N_MOD, D)
        f1 = (ffn_w_gate[l, 0], ffn_w_up[l, 0], ffn_w_down[l, 0])
        f2 = (ffn_w_gate[l, 1], ffn_w_up[l, 1], ffn_w_down[l, 1])
        hy = (hyena_conv_w[l], hyena_conv_b[l], hyena_f_w1[l], hyena_f_b1[l], hyena_f_w2[l],
              hyena_f_b2[l], hyena_f_w3[l], hyena_sin_freq[l], hyena_bias[l])

        h = ffn_half_step(h, norm_g[l, 0], mx[:, 0], mx[:, 1], mx[:, 2], *f1)
        hc = ffn_half_step(hc, norm_g[l, 0], mc[0], mc[1], mc[2], *f1)

        ux = modulate(rms_norm(h, norm_g[l, 1]), mx[:, 3], mx[:, 4])
        uc = modulate(rms_norm(hc, norm_g[l, 1]), mc[3], mc[4])
        px = ux @ w_in[l]
        pc = uc @ (w_in[l][:, K_OFF:] if last else w_in[l])

        qx = attn_heads(px[..., Q_OFF:K_OFF], N_Q_HEADS, q_norm_g[l], cos, sin)
        qx = qx.reshape(B, L, N_KV_HEADS, GQA_GROUP, HEAD_DIM)
        kx = attn_heads(px[..., K_OFF:V_OFF], N_KV_HEADS, k_norm_g[l], cos, sin)
        vx = px[..., V_OFF:].reshape(B, L, N_KV_HEADS, HEAD_DIM)
        pc_kv = pc[..., -2 * KV_WIDTH:]
        kc = attn_heads(pc_kv[..., :KV_WIDTH], N_KV_HEADS, k_norm_g[l], None, None)
        vc = pc_kv[..., KV_WIDTH:].reshape(B, C, N_KV_HEADS, HEAD_DIM)

        k_all = jnp.concatenate([kx, kc], axis=1)
        v_all = jnp.concatenate([vx, vc], axis=1)
        mix_x = jnp.concatenate([
            multiscale_pool(px[..., :HYENA_OFF], pool_w[l], pool_scale[l]),
            hyena_mixer(px[..., HYENA_OFF:Q_OFF], *hy),
            blocked_attention(qx, k_all, v_all),
        ], axis=-1) @ w_out[l]
        h = h + mx[:, 5] * mix_x

        if not last:
            qc = attn_heads(pc[..., Q_OFF:K_OFF], N_Q_HEADS, q_norm_g[l], None, None)
            qc = qc.reshape(B, C, N_KV_HEADS, GQA_GROUP, HEAD_DIM)
            mix_c = jnp.concatenate([
                multiscale_pool(pc[..., :HYENA_OFF], pool_w[l], pool_scale[l]),
                hyena_mixer(pc[..., HYENA_OFF:Q_OFF], *hy),
                attend(qc, kc, vc),
            ], axis=-1) @ w_out[l]
            hc = hc + mc[5] * mix_c
            hc = ffn_half_step(hc, norm_g[l, 2], mc[6], mc[7], mc[8], *f2)

        h = ffn_half_step(h, norm_g[l, 2], mx[:, 6], mx[:, 7], mx[:, 8], *f2)
    return h
```

```python
import math
from contextlib import ExitStack
import numpy as np
import ml_dtypes
import concourse.bass as bass
import concourse.mybir as mybir
from concourse.bass_utils import run_bass_kernel_spmd

F32 = mybir.dt.float32
BF16 = mybir.dt.bfloat16
ALU = mybir.AluOpType
AF = mybir.ActivationFunctionType

EPS = 1e-6
POOL_WINDOWS = (2, 4, 8, 16)
HY_OFF, Q_OFF, K_OFF, V_OFF, IN_W = 256, 1024, 1536, 1664, 1792
GRID_W = 64
ATTN_SCALE = 64 ** -0.5
PI_S = 3.1415925
TWO_PI = 2.0 * math.pi
SB_WORDS = 49152


class Cfg:
    def __init__(s, D=1024, L=2048, C=256, DFF=2816, DEPTH=4, NG=2):
        s.D, s.L, s.C, s.DFF, s.DEPTH, s.NG = D, L, C, DFF, DEPTH, NG
        s.DC = D // 128
        s.LT = L // 128
        s.CT = C // 128
        s.T = L + C
        s.TT = s.T // 128
        s.FC = DFF // 128


class Sem:
    def __init__(s, h):
        s.h = h
        s.total = 0


class Ev:
    __slots__ = ("sem", "val")

    def __init__(s, sem, val):
        s.sem = sem
        s.val = val


class Tile:
    def __init__(s, name, ap):
        s.name = name
        s.ap = ap
        s.w = None
        s.r = {}
        s.dsem = {}


class Eng:
    def __init__(s, name, h, sem):
        s.name, s.h, s.sem = name, h, sem
        s.waited = {}
        s.pending = []


def _reshape(ap, free_shape):
    if len(free_shape) <= 1:
        return ap
    names = [f"a{i}" for i in range(len(free_shape))]
    pat = "p (" + " ".join(names) + ") -> p " + " ".join(names)
    kw = {n: int(v) for n, v in zip(names[:-1], free_shape[:-1])}
    return ap.rearrange(pat, **kw)


class Prog:
    def __init__(s, nc, stack, nds=88):
        s.nc = nc

        def mk(name):
            return Sem(stack.enter_context(nc.semaphore(name)))

        s.pe = Eng("pe", nc.tensor, mk("s_pe"))
        s.act = Eng("act", nc.scalar, mk("s_act"))
        s.dve = Eng("dve", nc.vector, mk("s_dve"))
        s.pool = Eng("pool", nc.gpsimd, mk("s_pool"))
        s.sp = Eng("sp", nc.sync, mk("s_sp"))
        s.engs = [s.pe, s.act, s.dve, s.pool, s.sp]
        s.bar = mk("s_bar")
        s.free_dsems = {"hw": [mk(f"d{i}") for i in range(nds - 24)], "sw": [mk(f"w{i}") for i in range(24)]}
        s.owners = []
        s.tiles = []
        s.sb = stack.enter_context(nc.sbuf_tensor("sbpool", [128, SB_WORDS], F32))
        s.ps = stack.enter_context(nc.psum_tensor("pspool", [128, 4096], F32))
        s.pbot = 0
        s.ttop = SB_WORDS
        s.cur_phase = None
        s.phase_count = 0
        s.mute = None
        s.bank = [s.reg(Tile(f"bank{i}", s.ps[:, i * 512:(i + 1) * 512])) for i in range(8)]

    def reg(s, t):
        s.tiles.append(t)
        return t

    def dram(s, name, ap):
        return s.reg(Tile(name, ap))

    def _carve(s, name, off, free_shape, dtype):
        n = int(np.prod(free_shape))
        words = n if dtype == F32 else (n + 1) // 2
        ap = s.sb[:, off:off + words]
        if dtype != F32:
            ap = ap.bitcast(dtype)
            if 2 * words != n:
                ap = ap[:, 0:n]
        return s.reg(Tile(name, _reshape(ap, list(free_shape))))

    @staticmethod
    def _words(free_shape, dtype):
        n = int(np.prod(free_shape))
        return n if dtype == F32 else (n + 1) // 2

    def palloc(s, name, free_shape, dtype):
        w = s._words(free_shape, dtype)
        off = s.pbot
        s.pbot += w
        assert s.pbot <= s.ttop, f"SBUF overflow (persist) at {name}: {s.pbot} > {s.ttop}"
        return s._carve(name, off, free_shape, dtype)

    def talloc(s, name, free_shape, dtype):
        w = s._words(free_shape, dtype)
        s.ttop -= w
        assert s.pbot <= s.ttop, f"SBUF overflow (transient) at {name}: {s.pbot} > {s.ttop}"
        return s._carve(name, s.ttop, free_shape, dtype)

    def need(s, eng, ev):
        if ev is None:
            return
        if eng is s.pe and ev.sem is s.pe.sem:
            return
        assert ev.val is not None, "unresolved PE event"
        k = id(ev.sem)
        if eng.waited.get(k, 0) >= ev.val:
            return
        eng.h.wait_ge(ev.sem.h, ev.val)
        eng.waited[k] = ev.val

    def deps(s, eng, reads, writes):
        for t in reads:
            s.need(eng, t.w)
        for t in writes:
            s.need(eng, t.w)
            for e in list(t.r.values()):
                s.need(eng, e)

    def mark(s, ev, reads, writes):
        for t in reads:
            t.r[id(ev.sem)] = ev
        for t in writes:
            t.w = ev
            t.r = {}

    def phase(s, name):
        s.cur_phase = name
        s.phase_count = 0

    def _muted(s, eng):
        if s.mute is not None and s.mute[0] == s.cur_phase:
            if s.phase_count >= s.mute[1] and not (eng is s.pe and s.pe.pending):
                return True
            s.phase_count += 1
        return False

    def op(s, eng, fn, reads=(), writes=(), inc=True):
        if s._muted(eng):
            return
        s.deps(eng, reads, writes)
        ins = fn()
        if inc:
            eng.sem.total += 1
            ins.then_inc(eng.sem.h, 1)
            ev = Ev(eng.sem, eng.sem.total)
            for p in eng.pending:
                p.val = eng.sem.total
            eng.pending = []
        else:
            ev = Ev(eng.sem, None)
            eng.pending.append(ev)
        s.mark(ev, reads, writes)

    def dma(s, q, pairs, reads, writes, owner):
        if s._muted(q):
            return
        s.deps(q, reads, writes)
        kind = "sw" if q is s.pool else "hw"
        if kind not in owner.dsem:
            assert s.free_dsems[kind], "out of DMA semaphores"
            if not owner.dsem:
                s.owners.append(owner)
            owner.dsem[kind] = s.free_dsems[kind].pop()
        sem = owner.dsem[kind]
        if sem.total > 0:
            s.need(q, Ev(sem, sem.total))
        for (o, i) in pairs:
            q.h.dma_start(out=o, in_=i).then_inc(sem.h, 16)
            sem.total += 16
        s.mark(Ev(sem, sem.total), reads, writes)

    def barrier(s):
        sp = s.sp
        for e in s.engs:
            assert not e.pending
            if e is not sp and e.sem.total > 0:
                s.need(sp, Ev(e.sem, e.sem.total))
        for o in s.owners:
            for kind, sem in o.dsem.items():
                s.need(sp, Ev(sem, sem.total))
                s.free_dsems[kind].append(sem)
            o.dsem = {}
        s.owners = []
        s.bar.total += 1
        sp.h.sem_inc(s.bar.h, 1)
        for e in s.engs:
            if e is not sp:
                e.h.wait_ge(s.bar.h, s.bar.total)
        for t in s.tiles:
            t.w = None
            t.r = {}
        s.ttop = SB_WORDS


def _bf(a):
    return np.ascontiguousarray(a.astype(np.float32)).astype(ml_dtypes.bfloat16)


def host_constants(cfg):
    c = {}
    c["identb"] = _bf(np.eye(128))
    c["onesf"] = np.ones((128, 128), np.float32)
    bo = np.zeros((128, 128), np.float32)
    bo[:64, :64] = 1.0
    bo[64:, 64:] = 1.0
    c["blockones"] = bo
    R = np.zeros((64, 64), np.float32)
    for dp in range(64):
        q = dp // 16
        if q in (0, 2):
            R[dp + 16, dp] = -1.0
        else:
            R[dp - 16, dp] = 1.0
    rm = np.zeros((128, 128), np.float32)
    rm[:64, :64] = R
    rm[64:, 64:] = R
    c["rotm"] = rm
    c["blockonesb"] = _bf(bo)
    c["rotmb"] = _bf(rm)
    L = cfg.L
    rows = L // GRID_W
    row = np.repeat(np.arange(rows), GRID_W).astype(np.float32)
    col = np.tile(np.arange(GRID_W), rows).astype(np.float32)
    inv_freq = (1.0 / (np.float32(10000.0) ** (np.arange(0, 32, 2, dtype=np.float32) / np.float32(32)))).astype(np.float32)
    ang_r = row[:, None] * inv_freq
    ang_c = col[:, None] * inv_freq
    ang = np.concatenate([ang_r, ang_r, ang_c, ang_c], axis=-1).astype(np.float32)
    cosT = np.cos(ang).T.astype(np.float32)
    sinT = np.sin(ang).T.astype(np.float32)
    c["ropecos"] = np.ascontiguousarray(np.concatenate([cosT, cosT], 0))
    c["ropesin"] = np.ascontiguousarray(np.concatenate([sinT, sinT], 0))
    invw = np.zeros((128, 2), np.float32)
    invc = np.zeros((128, 2, 16), np.float32)
    for ci in range(2):
        for p in range(128):
            w = POOL_WINDOWS[2 * ci + p // 64]
            invw[p, ci] = 1.0 / w
            for e in range(16):
                if e < 8:
                    t = e
                    lo = max(t - w // 2, 0)
                    hi = t - w // 2 + w
                else:
                    d = 8 - (e - 8)
                    lo = -d - w // 2
                    hi = min(-d - w // 2 + w, 0)
                invc[p, ci, e] = 1.0 / float(hi - lo)
    c["invw"] = invw
    c["invc"] = invc.reshape(128, 32)
    for tag, Lx in (("L", cfg.L), ("C", cfg.C)):
        f32 = np.float32
        t = np.linspace(0.0, 1.0, Lx, dtype=f32)[:, None]
        bands = 16
        w_ang = (f32(2.0 * math.pi) * np.arange(Lx, dtype=f32) / f32(Lx)).astype(f32)
        freqs = np.linspace(1e-4, bands - 1, bands, dtype=f32)
        ang = (w_ang[:, None] * freqs[None, :]).astype(f32)
        z = np.concatenate([t, np.cos(ang), -np.sin(ang)], axis=-1).astype(f32)
        c["zT_" + tag] = np.ascontiguousarray(z.T)
        max_decay = math.log(1e-2) / 0.3
        min_decay = math.log(1e-2) / 1.5
        deltas = np.linspace(min_decay, max_decay, 256, dtype=f32)
        c["decay_" + tag] = np.exp(-t * np.abs(deltas)).astype(f32)
        N = 2 * Lx
        nt = Lx // 128
        s_idx = np.arange(Lx, dtype=np.float64)[:, None]
        th = 2.0 * np.pi * (np.arange(Lx, dtype=np.float64) + 0.5) / N
        Fc = np.cos(s_idx * th[None, :])
        Fs = -np.sin(s_idx * th[None, :])
        Fm = np.concatenate([Fc, Fs], axis=1)
        Fh = Fm.reshape(nt, 128, 2 * nt, 128).transpose(2, 1, 0, 3).reshape(2 * nt, 128, nt * 128)
        c["Fh_" + tag] = _bf(Fh)
        c["Gh_" + tag] = _bf((2.0 / N) * Fm.T)
    return c


CONST_SHAPES = None


def build_program(cfg, consts):
    nc = bass.Bass("TRN2", target_bir_lowering=False)
    D, L, C, DFF, DEPTH = cfg.D, cfg.L, cfg.C, cfg.DFF, cfg.DEPTH
    DC, LT, CT, T, TT, FC = cfg.DC, cfg.LT, cfg.CT, cfg.T, cfg.TT, cfg.FC

    def din(name, shape, dt=F32):
        return nc.dram_tensor(name, list(shape), dt, kind="ExternalInput").ap()

    x = din("x", [L, D])
    ctx = din("ctx", [C, D])
    c_in = din("c", [1, D])
    cc_in = din("c_ctx", [1, D])
    norm_g = din("norm_g", [DEPTH, 3, D])
    w_mod = din("w_mod", [DEPTH, D, 9 * D])
    b_mod = din("b_mod", [DEPTH, 9 * D])
    w_gate = din("ffn_w_gate", [DEPTH, 2, D, DFF])
    w_up = din("ffn_w_up", [DEPTH, 2, D, DFF])
    w_down = din("ffn_w_down", [DEPTH, 2, DFF, D])
    w_in = din("w_in", [DEPTH, D, IN_W])
    w_out = din("w_out", [DEPTH, 1024, D])
    pool_w = din("pool_w", [DEPTH, 4, 64, 64])
    pool_scale = din("pool_scale", [DEPTH, 256])
    hconv_w = din("hyena_conv_w", [DEPTH, 3, 768])
    hconv_b = din("hyena_conv_b", [DEPTH, 768])
    f_w1 = din("hyena_f_w1", [DEPTH, 33, 64])
    f_b1 = din("hyena_f_b1", [DEPTH, 64])
    f_w2 = din("hyena_f_w2", [DEPTH, 64, 64])
    f_b2 = din("hyena_f_b2", [DEPTH, 64])
    f_w3 = din("hyena_f_w3", [DEPTH, 64, 1024])
    sin_freq = din("hyena_sin_freq", [DEPTH, 2, 64])
    hy_bias = din("hyena_bias", [DEPTH, 2, 256])
    q_norm_g = din("q_norm_g", [DEPTH, 64])
    k_norm_g = din("k_norm_g", [DEPTH, 64])
    cst = {}
    for k, v in consts.items():
        cst[k] = din("k_" + k, v.shape, BF16 if v.dtype == ml_dtypes.bfloat16 else F32)
    out = nc.dram_tensor("out", [L, D], F32, kind="ExternalOutput").ap()
    Hs = nc.dram_tensor("Hs", [T, D], F32, kind="Internal").ap()
    modv = nc.dram_tensor("modv", [DEPTH, 2, 9 * D], F32, kind="Internal").ap()
    Kf = {}
    for l in range(DEPTH):
        for s_, Lx in ((0, L), (1, C)):
            Kf[(l, s_)] = nc.dram_tensor(f"Kf{l}_{s_}", [2 * Lx, 512], F32, kind="Internal").ap()

    stack = ExitStack()
    with stack:
        stack.enter_context(nc.allow_non_contiguous_dma(reason="small strided parameter loads"))
        P = Prog(nc, stack)
        pe, act, dve, pool, sp = P.pe, P.act, P.dve, P.pool, P.sp
        V, S_, G_, PE_ = nc.vector, nc.scalar, nc.gpsimd, nc.tensor

        Hd = [P.dram(f"H{j}", Hs[j * 128:(j + 1) * 128, :]) for j in range(TT)]
        Xd = [P.dram(f"X{j}", (x[j * 128:(j + 1) * 128, :] if j < LT else ctx[(j - LT) * 128:(j - LT + 1) * 128, :]))
              for j in range(TT)]
        Od = [P.dram(f"O{j}", out[j * 128:(j + 1) * 128, :]) for j in range(LT)]
        dmisc = P.dram("dmisc", Hs)
        modv_t = [P.dram(f"modv{l}", modv[l]) for l in range(DEPTH)]
        Kf_t = {k: P.dram(f"Kft{k}", v) for k, v in Kf.items()}

        identb = P.palloc("identb", [128], BF16)
        onesf = P.palloc("onesf", [128], F32)
        blockones = P.palloc("blockones", [128], F32)
        rotm = P.palloc("rotm", [128], F32)
        blockonesb = P.palloc("blockonesb", [128], BF16)
        rotmb = P.palloc("rotmb", [128], BF16)
        epsc = P.palloc("epsc", [1], F32)
        cact = P.palloc("cact", [DC, 2], BF16)
        onesb = P.palloc("onesb", [128], BF16)
        invw = P.palloc("invw", [2], F32)
        invc = P.palloc("invc", [2, 16], F32)
        neg_half = None
        P.dma(sp, [(identb.ap, cst["identb"])], [dmisc], [identb], identb)
        P.dma(sp, [(onesf.ap, cst["onesf"])], [dmisc], [onesf], onesf)
        P.dma(sp, [(blockones.ap, cst["blockones"])], [dmisc], [blockones], blockones)
        P.dma(sp, [(rotm.ap, cst["rotm"])], [dmisc], [rotm], rotm)
        P.dma(sp, [(invw.ap, cst["invw"])], [dmisc], [invw], invw)
        P.dma(sp, [(blockonesb.ap, cst["blockonesb"])], [dmisc], [blockonesb], blockonesb)
        P.dma(sp, [(rotmb.ap, cst["rotmb"])], [dmisc], [rotmb], rotmb)
        P.op(dve, lambda: V.memset(epsc.ap, EPS), [], [epsc])
        P.op(dve, lambda: V.memset(onesb.ap, 1.0), [], [onesb])
        P.dma(sp, [(invc.ap, cst["invc"].rearrange("p (a b) -> p a b", a=2))], [dmisc], [invc], invc)
        base_mark = None

        def blocks_of(streams):
            res = []
            for s_ in streams:
                g0, Lx = (0, L) if s_ == 0 else (L, C)
                t = 0
                while t < Lx:
                    n = min(512, Lx - t)
                    res.append((g0 + t, n, s_, t))
                    t += n
            return res

        def blk_index(blocks, tglob):
            for i, (t0, n, s_, tl) in enumerate(blocks):
                if t0 <= tglob < t0 + n:
                    return i
            raise AssertionError

        def mod_setup():
            ca = P.talloc("ca", [DC], F32)
            cb = P.talloc("cb", [DC], F32)
            P.dma(sp, [(ca.ap, c_in[0, :].rearrange("(dc p) -> p dc", p=128))], [dmisc], [ca], ca)
            P.dma(sp, [(cb.ap, cc_in[0, :].rearrange("(dc p) -> p dc", p=128))], [dmisc], [cb], cb)
            P.op(act, lambda: S_.activation(out=cact.ap[:, :, 0], in_=ca.ap, func=AF.Silu), [ca], [cact])
            P.op(act, lambda: S_.activation(out=cact.ap[:, :, 1], in_=cb.ap, func=AF.Silu), [cb], [cact])

        def mod_emitters(l, bank_fn):
            J = 9 * D
            wms = [P.talloc(f"wm{i}", [DC, 512], BF16) for i in range(3)]
            bmt = [P.talloc(f"bm{i}", [512], F32) for i in range(2)]
            mvt = [P.talloc(f"mv{i}", [512], F32) for i in range(2)]
            wsrc = w_mod[l].rearrange("(dc p) j -> p dc j", p=128)
            blks = []
            j0 = 0
            while j0 < J:
                n = min(512, J - j0)
                blks.append((j0, n))
                j0 += n
            issued = [0]

            def prefetch(upto):
                while issued[0] <= min(upto, len(blks) - 1):
                    kk = issued[0]
                    jj, nn = blks[kk]
                    slot = wms[kk % 3]
                    P.dma(pool, [(slot.ap[:, :, 0:nn], wsrc[:, :, jj:jj + nn])], [dmisc], [slot], slot)
                    issued[0] += 1

            ems = []
            for k, (j0, n) in enumerate(blks):

                def emit(j0=j0, n=n, k=k):
                    slot, bm, mv, bank = wms[k % 3], bmt[k % 2], mvt[k % 2], bank_fn(k)
                    prefetch(k + 1)
                    P.dma(sp, [(bm.ap[0:2, 0:n], b_mod[l:l + 1, j0:j0 + n].to_broadcast([2, n]))], [dmisc], [bm], bm)
                    for dc in range(DC):
                        P.op(pe, lambda: PE_.matmul(bank.ap[0:2, 0:n], lhsT=cact.ap[:, dc, :], rhs=slot.ap[:, dc, 0:n],
                                                    start=(dc == 0), stop=(dc == DC - 1)),
                             [cact, slot], [bank], inc=(dc == DC - 1))
                    P.op(dve, lambda: V.tensor_tensor(out=mv.ap[0:2, 0:n], in0=bank.ap[0:2, 0:n], in1=bm.ap[0:2, 0:n], op=ALU.add),
                         [bank, bm], [mv])
                    P.dma(sp, [(modv[l][:, j0:j0 + n], mv.ap[0:2, 0:n])], [mv], [modv_t[l]], mv)
                    prefetch(k + 2)

                ems.append(emit)
            prefetch(1)
            return ems

        def phase_mod():
            mod_setup()
            for e in mod_emitters(0, lambda k: P.bank[k % 4]):
                e()
            P.barrier()

        def phase_filters_gen(l, s_):
            Lx = L if s_ == 0 else C
            tag = "L" if s_ == 0 else "C"
            nt = Lx // 128
            h2 = P.palloc("h2", [Lx], F32)
            w1 = P.talloc("fw1", [64], F32)
            w2 = P.talloc("fw2", [64], F32)
            vec = P.talloc("fvec", [8], F32)
            zT = P.talloc("zT", [Lx], F32)
            P.dma(sp, [(w1.ap[0:33, :], f_w1[l])], [dmisc], [w1], w1)
            P.dma(sp, [(w2.ap[0:64, :], f_w2[l])], [dmisc], [w2], w2)
            P.dma(sp, [(vec.ap[0:64, 0:1], f_b1[l, :].rearrange("(p o) -> p o", o=1)),
                       (vec.ap[0:64, 1:2], sin_freq[l, 0, :].rearrange("(p o) -> p o", o=1)),
                       (vec.ap[0:64, 2:3], f_b2[l, :].rearrange("(p o) -> p o", o=1)),
                       (vec.ap[0:64, 3:4], sin_freq[l, 1, :].rearrange("(p o) -> p o", o=1))], [dmisc], [vec], vec)
            P.dma(sp, [(zT.ap[0:33, :], cst["zT_" + tag])], [dmisc], [zT], zT)
            P.op(dve, lambda: V.tensor_tensor(out=vec.ap[0:64, 4:5], in0=vec.ap[0:64, 0:1], in1=vec.ap[0:64, 1:2], op=ALU.mult), [vec], [vec])
            P.op(dve, lambda: V.tensor_tensor(out=vec.ap[0:64, 5:6], in0=vec.ap[0:64, 2:3], in1=vec.ap[0:64, 3:4], op=ALU.mult), [vec], [vec])
            h1 = P.talloc("h1", [Lx], F32)
            tmps = [[P.talloc(f"ft{i}_{k}", [512], F32) for k in range(4)] for i in range(4)]
            cnt = 0

            def sin_layer(hin, K, wt, sfc, sfbc, hout):
                blks = []
                t0 = 0
                while t0 < Lx:
                    n = min(512, Lx - t0)
                    blks.append((t0, n))
                    t0 += n
                assert len(blks) <= 4
                E = list(enumerate(blks))
                for i, (t0, n) in E:
                    P.op(pe, lambda: PE_.matmul(P.bank[i].ap[0:64, 0:n], lhsT=wt.ap[0:K, 0:64], rhs=hin.ap[0:K, t0:t0 + n],
                                                start=True, stop=True), [wt, hin], [P.bank[i]])
                for i, (t0, n) in E:
                    a = tmps[i][0]
                    P.op(dve, lambda: V.tensor_scalar(out=a.ap[0:64, 0:n], in0=P.bank[i].ap[0:64, 0:n],
                                                      scalar1=vec.ap[0:64, sfc:sfc + 1], scalar2=vec.ap[0:64, sfbc:sfbc + 1],
                                                      op0=ALU.mult, op1=ALU.add), [P.bank[i], vec], [a])
                for i, (t0, n) in E:
                    a, m1, m2, a2 = tmps[i]
                    P.op(dve, lambda: V.tensor_scalar(out=m1.ap[0:64, 0:n], in0=a.ap[0:64, 0:n], scalar1=PI_S, scalar2=-TWO_PI,
                                                      op0=ALU.is_gt, op1=ALU.mult), [a], [m1])
                    P.op(dve, lambda: V.tensor_scalar(out=m2.ap[0:64, 0:n], in0=a.ap[0:64, 0:n], scalar1=-PI_S, scalar2=TWO_PI,
                                                      op0=ALU.is_lt, op1=ALU.mult), [a], [m2])
                for i, (t0, n) in E:
                    a, m1, m2, a2 = tmps[i]
                    P.op(dve, lambda: V.tensor_tensor(out=a2.ap[0:64, 0:n], in0=a.ap[0:64, 0:n], in1=m1.ap[0:64, 0:n], op=ALU.add), [a, m1], [a2])
                for i, (t0, n) in E:
                    a, m1, m2, a2 = tmps[i]
                    P.op(dve, lambda: V.tensor_tensor(out=a2.ap[0:64, 0:n], in0=a2.ap[0:64, 0:n], in1=m2.ap[0:64, 0:n], op=ALU.add), [a2, m2], [a2])
                for i, (t0, n) in E:
                    a, m1, m2, a2 = tmps[i]
                    P.op(dve, lambda: V.tensor_scalar(out=a2.ap[0:64, 0:n], in0=a2.ap[0:64, 0:n], scalar1=PI_S, scalar2=-PI_S,
                                                      op0=ALU.min, op1=ALU.max), [a2], [a2])
                for i, (t0, n) in E:
                    a, m1, m2, a2 = tmps[i]
                    P.op(act, lambda: S_.activation(out=hout.ap[0:64, t0:t0 + n], in_=a2.ap[0:64, 0:n], func=AF.Sin), [a2], [hout])

            sin_layer(zT, 33, w1, 1, 4, h1)
            sin_layer(h1, 64, w2, 3, 5, h2)
            yield
            kb = P.palloc("kb", [nt, 2, 2, 256], BF16)
            w3 = P.talloc("fw3", [1024], BF16)
            P.dma(pool, [(w3.ap[0:64, :], f_w3[l])], [dmisc], [w3], w3)
            h2b = P.talloc("h2b", [Lx], BF16)
            P.op(act, lambda: S_.copy(out=h2b.ap[0:64, :], in_=h2.ap[0:64, :]), [h2], [h2b])
            fl = P.talloc("fl", [nt, 2, 2, 256], F32)
            dec = P.talloc("dec", [nt, 256], F32)
            P.dma(sp, [(dec.ap, cst["decay_" + tag].rearrange("(tc p) c -> p tc c", p=128))], [dmisc], [dec], dec)
            for tc in range(nt):
                for o in range(2):
                    bank = P.bank[2 + (cnt % 2)]
                    cnt += 1
                    P.op(pe, lambda: PE_.matmul(bank.ap[:, 0:512], lhsT=h2b.ap[0:64, tc * 128:(tc + 1) * 128],
                                                rhs=w3.ap[0:64, o * 512:(o + 1) * 512], start=True, stop=True), [h2b, w3], [bank])
                    for d_ in range(2):
                        P.op(dve, lambda: V.tensor_tensor(out=fl.ap[:, tc, o, d_, :], in0=bank.ap[:, d_ * 256:(d_ + 1) * 256],
                                                          in1=dec.ap[:, tc, :], op=ALU.mult), [bank, dec], [fl])
            for o in range(2):
                P.op(pool, lambda: G_.memset(fl.ap[0:1, 0, o, 1, :], 0.0), [], [fl])
            absb = [P.talloc(f"absb{i}", [1024], BF16) for i in range(2)]
            cs0, cs1 = P.bank[4], P.bank[5]
            for tc in range(nt):
                ab = absb[tc % 2]
                flv = fl.ap[:, tc, :, :, :].rearrange("p o d c -> p (o d c)")
                P.op(dve, lambda: V.scalar_tensor_tensor(out=ab.ap, in0=flv, scalar=-1.0, in1=flv, op0=ALU.mult, op1=ALU.max), [fl], [ab])
                for o, bk in ((0, cs0), (1, cs1)):
                    P.op(pe, lambda: PE_.matmul(bk.ap[:, 0:512], lhsT=onesb.ap, rhs=ab.ap[:, o * 512:(o + 1) * 512],
                                                start=(tc == 0), stop=(tc == nt - 1)), [onesb, ab], [bk], inc=True)
            nrm = P.talloc("nrm", [2, 256], F32)
            rn = P.talloc("rn", [2, 256], F32)
            for o, bk in ((0, cs0), (1, cs1)):
                P.op(act, lambda: S_.copy(out=nrm.ap[:, o, :], in_=bk.ap[:, 0:256]), [bk], [nrm])
                P.op(dve, lambda: V.tensor_tensor(out=nrm.ap[:, o, :], in0=nrm.ap[:, o, :], in1=bk.ap[:, 256:512], op=ALU.add), [nrm, bk], [nrm])
            P.op(dve, lambda: V.reciprocal(out=rn.ap, in_=nrm.ap), [nrm], [rn])
            sdt = [P.talloc(f"sdt{i}", [2, 256], F32) for i in range(4)]
            for tc in range(nt):
                for j_, op_ in ((0, ALU.add), (1, ALU.subtract)):
                    P.op(dve, lambda: V.tensor_tensor(out=kb.ap[:, tc, j_, :, :], in0=fl.ap[:, tc, :, 0, :], in1=fl.ap[:, tc, :, 1, :], op=op_), [fl], [kb])
            rnp = P.palloc("rnp", [2, 256], F32)
            P.op(dve, lambda: V.tensor_copy(out=rnp.ap, in_=rn.ap), [rn], [rnp])
            yield
            Fsl = [[P.talloc(f"F{i}_{k}", [nt * 128], BF16) for k in range(2)] for i in range(2)]
            kouts = [P.talloc(f"kout{i}", [2, 2, 256], F32) for i in range(2)]
            Fh = cst["Fh_" + tag]

            def load_F(fc):
                Fc_, Fs_ = Fsl[fc % 2]
                P.dma(sp, [(Fc_.ap, Fh[fc])], [dmisc], [Fc_], Fc_)
                P.dma(sp, [(Fs_.ap, Fh[nt + fc])], [dmisc], [Fs_], Fs_)

            load_F(0)
            for fc in range(nt):
                Fc_, Fs_ = Fsl[fc % 2]
                if fc + 1 < nt:
                    load_F(fc + 1)
                bks = [P.bank[2 * (fc % 4) + i] for i in range(2)]
                for sc in range(nt):
                    for ti, Ft in enumerate((Fc_, Fs_)):
                        bk = bks[ti]
                        P.op(pe, lambda: PE_.matmul(bk.ap[:, 0:512], lhsT=Ft.ap[:, sc * 128:(sc + 1) * 128],
                                                    rhs=kb.ap[:, sc, ti, :, :].rearrange("p o c -> p (o c)"),
                                                    start=(sc == 0), stop=(sc == nt - 1)), [Ft, kb], [bk], inc=(sc == nt - 1))
                ko = kouts[fc % 2]
                rnv = rnp.ap.rearrange("p o c -> p (o c)")
                P.op(dve, lambda: V.tensor_tensor(out=ko.ap[:, 0, :, :].rearrange("p o c -> p (o c)"), in0=bks[0].ap[:, 0:512], in1=rnv, op=ALU.mult),
                     [bks[0], rnp], [ko])
                P.op(dve, lambda: V.tensor_tensor(out=ko.ap[:, 1, :, :].rearrange("p o c -> p (o c)"), in0=bks[1].ap[:, 0:512], in1=rnv, op=ALU.mult),
                     [bks[1], rnp], [ko])
                kf = Kf[(l, s_)]
                P.dma(sp, [(kf[fc * 128:(fc + 1) * 128, :], ko.ap[:, 0, :, :].rearrange("p o c -> p (o c)")),
                           (kf[(nt + fc) * 128:(nt + fc + 1) * 128, :], ko.ap[:, 1, :, :].rearrange("p o c -> p (o c)"))],
                      [ko], [Kf_t[(l, s_)]], ko)

        def phase_filters(l, streams):
            pm_f = P.pbot
            alive = [phase_filters_gen(l, s_) for s_ in streams]
            while alive:
                nxt = []
                for g in alive:
                    try:
                        next(g)
                        nxt.append(g)
                    except StopIteration:
                        pass
                P.barrier()
                alive = nxt
            P.pbot = pm_f

        def build_bc(l, which, streams, persist_gate, half_gate):
            ks, kscale, kg = 3 * which, 3 * which + 1, 3 * which + 2
            Cc = {}
            for s_ in streams:
                Cc[s_] = (P.palloc if persist_gate else P.talloc)(f"Cc{s_}", [D], F32)
                P.dma(sp, [(Cc[s_].ap, modv[l, s_:s_ + 1, kg * D:(kg + 1) * D].to_broadcast([128, D]))], [modv_t[l]], [Cc[s_]], Cc[s_])
                if half_gate:
                    P.op(dve, lambda: V.tensor_scalar(out=Cc[s_].ap, in0=Cc[s_].ap, scalar1=0.5, scalar2=None, op0=ALU.mult),
                         [Cc[s_]], [Cc[s_]])
            return Cc

        def build_cols(l, which, streams):
            ks, kscale = 3 * which, 3 * which + 1
            gc = P.talloc("gcl", [DC], F32)
            P.dma(sp, [(gc.ap, norm_g[l, which, :].rearrange("(dc p) -> p dc", p=128))], [dmisc], [gc], gc)
            gcol, scol = {}, {}
            for s_ in streams:
                sct = P.talloc(f"sctc{s_}", [DC], F32)
                gcol[s_] = P.talloc(f"gcol{s_}", [DC], F32)
                scol[s_] = P.talloc(f"scol{s_}", [DC], F32)
                P.dma(sp, [(sct.ap, modv[l, s_, kscale * D:(kscale + 1) * D].rearrange("(dc p) -> p dc", p=128))], [modv_t[l]], [sct], sct)
                P.dma(sp, [(scol[s_].ap, modv[l, s_, ks * D:(ks + 1) * D].rearrange("(dc p) -> p dc", p=128))], [modv_t[l]], [scol[s_]], scol[s_])
                P.op(dve, lambda: V.scalar_tensor_tensor(out=gcol[s_].ap, in0=sct.ap, scalar=1.0, in1=gc.ap, op0=ALU.add, op1=ALU.mult),
                     [sct, gc], [gcol[s_]])
            return gcol, scol

        def phase_norm(tiles, srcs, gcol, scol, yT, yblk, blocks):
            NH, NY = 4, 8
            hbs = [P.talloc(f"nh{i}", [D], F32) for i in range(NH)]
            junk = P.talloc("njunk", [D], BF16)
            ybs = [P.talloc(f"nyb{i}", [D], BF16) for i in range(NY)]
            sss = [P.talloc(f"nss{i}", [1], F32) for i in range(4)]
            rss = [P.talloc(f"nrs{i}", [1], F32) for i in range(4)]
            rds = [P.talloc(f"nrd{i}", [1], F32) for i in range(4)]
            cnt = [0]
            ybof = {}

            def stage_a(j):
                i = cnt[0]
                cnt[0] += 1
                hb = hbs[i % NH]
                P.dma(sp, [(hb.ap, srcs[j].ap)], [srcs[j]], [hb], hb)
                ss, rs, rd, yb = sss[i % 4], rss[i % 4], rds[i % 4], ybs[i % NY]
                ybof[j] = yb
                P.op(act, lambda: S_.activation(out=junk.ap, in_=hb.ap, func=AF.Square, accum_out=ss.ap), [hb], [junk, ss])
                P.op(act, lambda: S_.activation(out=rs.ap, in_=ss.ap, func=AF.Sqrt, scale=1.0 / D, bias=EPS), [ss], [rs])
                P.op(dve, lambda: V.reciprocal(out=rd.ap, in_=rs.ap), [rs], [rd])
                P.op(dve, lambda: V.tensor_scalar(out=yb.ap, in0=hb.ap, scalar1=rd.ap, scalar2=None, op0=ALU.mult), [hb, rd], [yb])

            def stage_b(b):
                t0, n, s_, tl = blocks[b]
                js = list(range(t0 // 128, (t0 + n) // 128))
                for dc in range(DC):
                    bank = P.bank[(b % 2) * 4 + (dc // 2) % 4]
                    bview = bank.ap.bitcast(BF16)
                    c0 = (dc % 2) * 512
                    for qi_, j in enumerate(js):
                        yb = ybof[j]
                        P.op(pe, lambda: PE_.transpose(out=bview[:, c0 + qi_ * 128:c0 + (qi_ + 1) * 128], in_=yb.ap[:, dc * 128:(dc + 1) * 128],
                                                       identity=identb.ap), [yb, identb], [bank], inc=(qi_ == len(js) - 1))
                    if dc % 3 == 0:
                        P.op(act, lambda: S_.activation(out=yT.ap[:, dc, t0:t0 + n], in_=bview[:, c0:c0 + n], func=AF.Identity,
                                                        scale=gcol[s_].ap[:, dc:dc + 1], bias=scol[s_].ap[:, dc:dc + 1]),
                             [bank, gcol[s_], scol[s_]], [yblk[b]])
                    else:
                        P.op(dve, lambda: V.tensor_scalar(out=yT.ap[:, dc, t0:t0 + n], in0=bview[:, c0:c0 + n], scalar1=gcol[s_].ap[:, dc:dc + 1],
                                                          scalar2=scol[s_].ap[:, dc:dc + 1], op0=ALU.mult, op1=ALU.add),
                             [bank, gcol[s_], scol[s_]], [yblk[b]])

            for b, (t0, n, s_, tl) in enumerate(blocks):
                for j in range(t0 // 128, (t0 + n) // 128):
                    stage_a(j)
                if b >= 1:
                    stage_b(b - 1)
            stage_b(len(blocks) - 1)

        def sublayer_ffn(l, w):
            last = (l == DEPTH - 1)
            final = last and w == 1
            streams = [0] if final else [0, 1]
            which = 0 if w == 0 else 2
            tiles = [(j, 0) for j in range(LT)] + ([] if final else [(j, 1) for j in range(LT, TT)])
            blocks = blocks_of(streams)
            Teff = sum(b[1] for b in blocks)
            srcs = Xd if (l == 0 and w == 0) else Hd
            pm = P.pbot
            yT = P.palloc("yT", [DC, Teff], BF16)
            yblk = [P.reg(Tile(f"yblk{i}", yT.ap)) for i in range(len(blocks))]
            Cc = build_bc(l, which, streams, True, True)
            gcol, scol = build_cols(l, which, streams)
            phase_norm(tiles, srcs, gcol, scol, yT, yblk, blocks)
            P.barrier()
            if stop == "ffnnorm":
                P.pbot = pm
                return
            groups = [list(a) for a in np.array_split(np.arange(FC), cfg.NG)]
            maxg = max(len(g) for g in groups)
            aT = P.talloc("aT", [maxg, Teff], BF16)
            aTt = [P.reg(Tile(f"aT{i}", aT.ap)) for i in range(len(blocks))]
            Wd = [P.talloc(f"Wd{k}", [D], BF16) for k in range(maxg)]
            wgs = [P.talloc(f"wg{i}", [DC, 128], BF16) for i in range(3)]
            wus = [P.talloc(f"wu{i}", [DC, 128], BF16) for i in range(3)]
            sgs = [P.talloc(f"sg{i}", [512], F32) for i in range(2)]
            tts = [P.talloc(f"tt{i}", [512], F32) for i in range(2)]
            NHB = 4
            hbs = [P.talloc(f"fh{i}", [D], F32) for i in range(NHB)]
            wgsrc = w_gate[l, w].rearrange("(dc p) f -> p dc f", p=128)
            wusrc = w_up[l, w].rearrange("(dc p) f -> p dc f", p=128)
            cntw = cntb = cnto = 0
            nhalf = (D + 511) // 512
            for g, fcs in enumerate(groups):
                for k, fc in enumerate(fcs):
                    fc = int(fc)
                    wg, wu = wgs[cntw % 3], wus[cntw % 3]
                    cntw += 1
                    P.dma(pool, [(wg.ap, wgsrc[:, :, fc * 128:(fc + 1) * 128])], [dmisc], [wg], wg)
                    P.dma(pool, [(wu.ap, wusrc[:, :, fc * 128:(fc + 1) * 128])], [dmisc], [wu], wu)
                    P.dma(pool, [(Wd[k].ap, w_down[l, w][fc * 128:(fc + 1) * 128, :])], [dmisc], [Wd[k]], Wd[k])
                    for b, (t0, n, s_, tl) in enumerate(blocks):
                        lt0 = t0 if not final else t0
                        bg, bu = P.bank[2 * (cntb % 2)], P.bank[2 * (cntb % 2) + 1]
                        for dc in range(DC):
                            P.op(pe, lambda: PE_.matmul(bg.ap[:, 0:n], lhsT=wg.ap[:, dc, :], rhs=yT.ap[:, dc, lt0:lt0 + n],
                                                        start=(dc == 0), stop=(dc == DC - 1)), [wg, yblk[b]], [bg], inc=(dc == DC - 1))
                        for dc in range(DC):
                            P.op(pe, lambda: PE_.matmul(bu.ap[:, 0:n], lhsT=wu.ap[:, dc, :], rhs=yT.ap[:, dc, lt0:lt0 + n],
                                                        start=(dc == 0), stop=(dc == DC - 1)), [wu, yblk[b]], [bu], inc=(dc == DC - 1))
                        sg = sgs[cntb % 2]
                        cntb += 1
                        P.op(act, lambda: S_.activation(out=sg.ap[:, 0:n], in_=bg.ap[:, 0:n], func=AF.Silu), [bg], [sg])
                        P.op(dve, lambda: V.tensor_tensor(out=aT.ap[:, k, lt0:lt0 + n], in0=sg.ap[:, 0:n], in1=bu.ap[:, 0:n], op=ALU.mult),
                             [sg, bu], [aTt[b]])
                nk = len(fcs)
                loaded = 0

                def ensure_loaded(upto):
                    nonlocal loaded
                    while loaded <= min(upto, len(tiles) - 1):
                        jj, _ = tiles[loaded]
                        rdt = srcs[jj] if g == 0 else Hd[jj]
                        hbx = hbs[loaded % NHB]
                        P.dma(sp, [(hbx.ap, rdt.ap)], [rdt], [hbx], hbx)
                        loaded += 1

                for i, (j, s_) in enumerate(tiles):
                    ensure_loaded(i + 2)
                    hb = hbs[i % NHB]
                    bi = blk_index(blocks, j * 128)
                    for hf in range(nhalf):
                        n = min(512, D - hf * 512)
                        bo = P.bank[4 + (cnto % 4)]
                        tt = tts[cnto % 2]
                        cnto += 1
                        for k in range(nk):
                            P.op(pe, lambda: PE_.matmul(bo.ap[:, 0:n], lhsT=aT.ap[:, k, j * 128:(j + 1) * 128],
                                                        rhs=Wd[k].ap[:, hf * 512:hf * 512 + n], start=(k == 0), stop=(k == nk - 1)),
                                 [aTt[bi], Wd[k]], [bo], inc=(k == nk - 1))
                        P.op(dve, lambda: V.tensor_tensor(out=tt.ap[:, 0:n], in0=bo.ap[:, 0:n], in1=Cc[s_].ap[:, hf * 512:hf * 512 + n], op=ALU.mult),
                             [bo, Cc[s_]], [tt])
                        P.op(pool, lambda: G_.tensor_tensor(out=hb.ap[:, hf * 512:hf * 512 + n], in0=hb.ap[:, hf * 512:hf * 512 + n],
                                                            in1=tt.ap[:, 0:n], op=ALU.add), [hb, tt], [hb])
                    wr = Od[j] if (final and g == len(groups) - 1) else Hd[j]
                    P.dma(sp, [(wr.ap, hb.ap)], [hb], [wr], hb)
            P.barrier()
            P.pbot = pm

        def sublayer_mixer(l):
            last = (l == DEPTH - 1)
            streams_all = [0, 1]
            mstreams = [0] if last else [0, 1]
            tiles = [(j, 0) for j in range(LT)] + [(j, 1) for j in range(LT, TT)]
            blocks = blocks_of(streams_all)
            mblocks = blocks_of(mstreams)
            pm = P.pbot
            uT_mark = None
            ks_g = 5
            Cc = {}
            for s_ in mstreams:
                Cc[s_] = P.palloc(f"mCc{s_}", [D], F32)
                P.dma(sp, [(Cc[s_].ap, modv[l, s_:s_ + 1, ks_g * D:(ks_g + 1) * D].to_broadcast([128, D]))], [modv_t[l]], [Cc[s_]], Cc[s_])
            mix_pool = P.palloc("mix_pool", [2, T], BF16)
            zc = {s_: P.palloc(f"zc{s_}", [6, (L if s_ == 0 else C)], BF16) for s_ in mstreams}
            qT = P.palloc("qT", [4, T], BF16)
            kdup = P.palloc("kdup", [2, T], BF16)
            vtok = P.palloc("vtok", [TT, 2, 128], BF16)
            mark_u = P.pbot
            uT = P.palloc("uT", [DC, T], BF16)
            ublk = [P.reg(Tile(f"ublk{i}", uT.ap)) for i in range(len(blocks))]
            gcol, scol = build_cols(l, 1, streams_all)
            phase_norm(tiles, Hd, gcol, scol, uT, ublk, blocks)
            P.barrier()
            if stop == "m1":
                P.pbot = pm
                return

            wsrc = w_in[l].rearrange("(dc p) f -> p dc f", p=128)
            cntw = [0]
            cntb = [0]

            def load_w(slots, col0, ncols=128):
                wt = slots[cntw[0] % len(slots)]
                cntw[0] += 1
                P.dma(pool, [(wt.ap[:, :, 0:ncols], wsrc[:, :, col0:col0 + ncols])], [dmisc], [wt], wt)
                return wt

            def proj(wt, b, bank, ncols=128):
                t0, n, s_, tl = blocks[b]
                for dc in range(DC):
                    P.op(pe, lambda: PE_.matmul(bank.ap[0:ncols, 0:n], lhsT=wt.ap[:, dc, 0:ncols], rhs=uT.ap[:, dc, t0:t0 + n],
                                                start=(dc == 0), stop=(dc == DC - 1)), [wt, ublk[b]], [bank], inc=(dc == DC - 1))

            wis = [P.talloc(f"wi{i}", [DC, 128], BF16) for i in range(3)]
            pbuf = {s_: P.talloc(f"pbuf{s_}", [2, (L if s_ == 0 else C) + 16], F32) for s_ in mstreams}
            for s_ in mstreams:
                Lx = L if s_ == 0 else C
                P.op(pool, lambda: G_.memset(pbuf[s_].ap[:, :, 0:8], 0.0), [], [pbuf[s_]])
                P.op(pool, lambda: G_.memset(pbuf[s_].ap[:, :, 8 + Lx:16 + Lx], 0.0), [], [pbuf[s_]])
            for ci in range(2):
                wt = load_w(wis, ci * 128)
                for b, (t0, n, s_, tl) in enumerate(blocks):
                    if s_ not in mstreams:
                        continue
                    bank = P.bank[cntb[0] % 2]
                    cntb[0] += 1
                    proj(wt, b, bank)
                    P.op(act, lambda: S_.copy(out=pbuf[s_].ap[:, ci, 8 + tl:8 + tl + n], in_=bank.ap[:, 0:n]), [bank], [pbuf[s_]])
            Wbd = [P.talloc(f"Wbd{ci}", [128], BF16) for ci in range(2)]
            psc = P.talloc("psc", [2], F32)
            P.dma(sp, [(psc.ap, pool_scale[l, :].rearrange("(k p) -> p k", p=128))], [dmisc], [psc], psc)
            for ci in range(2):
                P.op(pool, lambda: G_.memset(Wbd[ci].ap, 0.0), [], [Wbd[ci]])
                P.dma(pool, [(Wbd[ci].ap[0:64, 0:64], pool_w[l, 2 * ci]), (Wbd[ci].ap[64:128, 64:128], pool_w[l, 2 * ci + 1])],
                      [dmisc], [Wbd[ci]], Wbd[ci])
            for s_ in mstreams:
                Lx = L if s_ == 0 else C
                g0 = 0 if s_ == 0 else L
                tA = P.talloc(f"ptA{s_}", [Lx + 16], F32)
                tB = P.talloc(f"ptB{s_}", [Lx + 16], F32)
                Sf = P.talloc(f"pSf{s_}", [Lx], F32)
                pooled = P.talloc(f"pooled{s_}", [2, Lx], BF16)
                etmp = P.talloc(f"petmp{s_}", [8], F32)
                for ci in range(2):
                    for hh in range(2):
                        w_ = POOL_WINDOWS[2 * ci + hh]
                        pr = slice(hh * 64, (hh + 1) * 64)
                        off = 8 - w_ // 2
                        length = Lx + w_ - 1
                        cur = pbuf[s_].ap[pr, ci, off:off + length]
                        cur_t = pbuf[s_]
                        m = 1
                        bi_ = 0
                        while m < w_:
                            newlen = length - m
                            fin = (2 * m == w_)
                            dst_t = Sf if fin else (tA, tB)[bi_]
                            dst = dst_t.ap[pr, 0:newlen]
                            P.op(dve, lambda: V.tensor_tensor(out=dst, in0=cur[:, 0:newlen], in1=cur[:, m:m + newlen], op=ALU.add),
                                 [cur_t], [dst_t])
                            cur, cur_t, length = dst, dst_t, newlen
                            m *= 2
                            bi_ ^= 1
                    P.op(dve, lambda: V.scalar_tensor_tensor(out=pooled.ap[:, ci, :], in0=Sf.ap, scalar=invw.ap[:, ci:ci + 1],
                                                             in1=pbuf[s_].ap[:, ci, 8:8 + Lx], op0=ALU.mult, op1=ALU.subtract),
                         [Sf, invw, pbuf[s_]], [pooled])
                    for (c0, e0) in ((0, 0), (Lx - 8, 8)):
                        P.op(dve, lambda: V.tensor_tensor(out=etmp.ap, in0=Sf.ap[:, c0:c0 + 8], in1=invc.ap[:, ci, e0:e0 + 8], op=ALU.mult),
                             [Sf, invc], [etmp])
                        P.op(dve, lambda: V.tensor_tensor(out=pooled.ap[:, ci, c0:c0 + 8], in0=etmp.ap,
                                                          in1=pbuf[s_].ap[:, ci, 8 + c0:16 + c0], op=ALU.subtract), [etmp, pbuf[s_]], [pooled])
                    t0 = 0
                    while t0 < Lx:
                        n = min(512, Lx - t0)
                        bank = P.bank[2 + cntb[0] % 2]
                        cntb[0] += 1
                        P.op(pe, lambda: PE_.matmul(bank.ap[:, 0:n], lhsT=Wbd[ci].ap, rhs=pooled.ap[:, ci, t0:t0 + n], start=True, stop=True),
                             [Wbd[ci], pooled], [bank])
                        P.op(dve, lambda: V.tensor_scalar(out=mix_pool.ap[:, ci, g0 + t0:g0 + t0 + n], in0=bank.ap[:, 0:n],
                                                          scalar1=psc.ap[:, ci:ci + 1], scalar2=None, op0=ALU.mult), [bank, psc], [mix_pool])
                        t0 += n
            P.barrier()
            if stop == "m2a":
                P.pbot = pm
                return

            wis = [P.talloc(f"wi{i}", [DC, 128], BF16) for i in range(3)]
            hp = {s_: P.talloc(f"hp{s_}", [6, (L if s_ == 0 else C) + 2], BF16) for s_ in mstreams}
            cw = P.talloc("cw", [6, 3], F32)
            cbv = P.talloc("cbv", [6], F32)
            P.dma(sp, [(cw.ap[:, :, k], hconv_w[l, k, :].rearrange("(ch p) -> p ch", p=128)) for k in range(3)], [dmisc], [cw], cw)
            P.dma(sp, [(cbv.ap, hconv_b[l, :].rearrange("(ch p) -> p ch", p=128))], [dmisc], [cbv], cbv)
            for s_ in mstreams:
                Lx = L if s_ == 0 else C
                P.op(pool, lambda: G_.memset(hp[s_].ap[:, :, 0:1], 0.0), [], [hp[s_]])
                P.op(pool, lambda: G_.memset(hp[s_].ap[:, :, Lx + 1:Lx + 2], 0.0), [], [hp[s_]])
            hpt = {s_: [P.reg(Tile(f"hpt{s_}_{ch}", hp[s_].ap)) for ch in range(6)] for s_ in mstreams}
            CH = 1024
            tas = [P.talloc(f"cta{i}", [CH], F32) for i in range(2)]
            tbs = [P.talloc(f"ctb{i}", [CH], F32) for i in range(2)]
            cc_ = 0
            for ch in range(6):
                wt = load_w(wis, HY_OFF + ch * 128)
                for b, (t0, n, s_, tl) in enumerate(blocks):
                    if s_ not in mstreams:
                        continue
                    bank = P.bank[cntb[0] % 4]
                    cntb[0] += 1
                    proj(wt, b, bank)
                    P.op(act, lambda: S_.copy(out=hp[s_].ap[:, ch, 1 + tl:1 + tl + n], in_=bank.ap[:, 0:n]), [bank], [hpt[s_][ch]])
                for s_ in mstreams:
                    Lx = L if s_ == 0 else C
                    t0 = 0
                    while t0 < Lx:
                        n = min(CH, Lx - t0)
                        ta, tb = tas[cc_ % 2], tbs[cc_ % 2]
                        cc_ += 1
                        P.op(dve, lambda: V.tensor_scalar(out=ta.ap[:, 0:n], in0=hp[s_].ap[:, ch, 1 + t0:1 + t0 + n], scalar1=cw.ap[:, ch, 1:2],
                                                          scalar2=cbv.ap[:, ch:ch + 1], op0=ALU.mult, op1=ALU.add), [hpt[s_][ch], cw, cbv], [ta])
                        P.op(dve, lambda: V.scalar_tensor_tensor(out=tb.ap[:, 0:n], in0=hp[s_].ap[:, ch, t0:t0 + n], scalar=cw.ap[:, ch, 0:1],
                                                                 in1=ta.ap[:, 0:n], op0=ALU.mult, op1=ALU.add), [hpt[s_][ch], cw, ta], [tb])
                        P.op(dve, lambda: V.scalar_tensor_tensor(out=zc[s_].ap[:, ch, t0:t0 + n], in0=hp[s_].ap[:, ch, 2 + t0:2 + t0 + n],
                                                                 scalar=cw.ap[:, ch, 2:3], in1=tb.ap[:, 0:n], op0=ALU.mult, op1=ALU.add),
                             [hpt[s_][ch], cw, tb], [zc[s_]])
                        t0 += n
            P.barrier()
            if stop == "m2b":
                P.pbot = pm
                return

            P.phase("m2c")
            wis = [P.talloc(f"wi{i}", [DC, 128], BF16) for i in range(3)]
            rcos = P.talloc("rcos", [L], F32)
            rsin = P.talloc("rsin", [L], F32)
            P.dma(sp, [(rcos.ap, cst["ropecos"])], [dmisc], [rcos], rcos)
            P.dma(sp, [(rsin.ap, cst["ropesin"])], [dmisc], [rsin], rsin)
            gq = P.talloc("gq", [1], F32)
            gk = P.talloc("gk", [1], F32)
            P.dma(sp, [(gq.ap[0:64, :], q_norm_g[l, :].rearrange("(p o) -> p o", o=1)),
                       (gq.ap[64:128, :], q_norm_g[l, :].rearrange("(p o) -> p o", o=1))], [dmisc], [gq], gq)
            P.dma(sp, [(gk.ap[0:64, :], k_norm_g[l, :].rearrange("(p o) -> p o", o=1)),
                       (gk.ap[64:128, :], k_norm_g[l, :].rearrange("(p o) -> p o", o=1))], [dmisc], [gk], gk)
            NR = 3
            NX = 5
            sqs = [P.talloc(f"qsq{i}", [512], BF16) for i in range(NR)]
            xss = [P.talloc(f"qxs{i}", [512], F32) for i in range(NX)]
            rst = [P.talloc(f"qrs{i}", [512], F32) for i in range(NR)]
            rdt = [P.talloc(f"qrd{i}", [512], F32) for i in range(NX)]
            xns = [P.talloc(f"qxn{i}", [512], BF16) for i in range(NX)]
            ats = [P.talloc(f"qat{i}", [512], F32) for i in range(NR)]
            bts = [P.talloc(f"qbt{i}", [512], F32) for i in range(NR)]
            work = []
            for qi in range(4):
                work.append(("q", qi))
            for kh in range(2):
                work.append(("k", kh))
            units = []
            for kind, idx in work:
                for b, (t0, n, s_, tl) in enumerate(blocks):
                    if kind == "q" and s_ not in mstreams:
                        continue
                    units.append((kind, idx, b))
            wcache = {}

            def get_w(kind, idx):
                if (kind, idx) in wcache:
                    return wcache[(kind, idx)]
                if kind == "q":
                    wt = load_w(wis, Q_OFF + idx * 128)
                else:
                    wt = wis[cntw[0] % 3]
                    cntw[0] += 1
                    P.dma(pool, [(wt.ap[:, :, 0:64], wsrc[:, :, K_OFF + idx * 64:K_OFF + (idx + 1) * 64]),
                                 (wt.ap[:, :, 64:128], wsrc[:, :, K_OFF + idx * 64:K_OFF + (idx + 1) * 64])], [dmisc], [wt], wt)
                wcache[(kind, idx)] = wt
                return wt

            def dest_of(kind, idx, a, b_):
                return (qT.ap[:, idx, a:b_], qT, gq) if kind == "q" else (kdup.ap[:, idx, a:b_], kdup, gk)

            def qk_p(u):
                kind, idx, b = units[u]
                proj(get_w(kind, idx), b, P.bank[u % 3])

            def qk_a(u):
                kind, idx, b = units[u]
                t0, n, s_, tl = blocks[b]
                r_ = u % NR
                bA, bB = P.bank[u % 3], P.bank[3 + u % 2]
                sq, xs, rs, rd = sqs[r_], xss[u % NX], rst[r_], rdt[u % NX]
                P.op(act, lambda: S_.copy(out=xs.ap[:, 0:n], in_=bA.ap[:, 0:n]), [bA], [xs])
                P.op(act, lambda: S_.activation(out=sq.ap[:, 0:n], in_=xs.ap[:, 0:n], func=AF.Square), [xs], [sq])
                P.op(pe, lambda: PE_.matmul(bB.ap[:, 0:n], lhsT=blockonesb.ap, rhs=sq.ap[:, 0:n], start=True, stop=True), [blockonesb, sq], [bB])
                P.op(act, lambda: S_.activation(out=rs.ap[:, 0:n], in_=bB.ap[:, 0:n], func=AF.Ln, scale=1.0 / 64, bias=epsc.ap), [bB, epsc], [rs])
                P.op(act, lambda: S_.activation(out=rd.ap[:, 0:n], in_=rs.ap[:, 0:n], func=AF.Exp, scale=-0.5), [rs], [rd])

            def qk_a2(u):
                kind, idx, b = units[u]
                t0, n, s_, tl = blocks[b]
                xs, rd, xn = xss[u % NX], rdt[u % NX], xns[u % NX]
                dst, dst_t, gvec = dest_of(kind, idx, t0, t0 + n)
                if s_ == 0:
                    P.op(dve, lambda: V.scalar_tensor_tensor(out=xn.ap[:, 0:n], in0=xs.ap[:, 0:n], scalar=gvec.ap, in1=rd.ap[:, 0:n],
                                                             op0=ALU.mult, op1=ALU.mult), [xs, gvec, rd], [xn])
                else:
                    P.op(dve, lambda: V.scalar_tensor_tensor(out=dst, in0=xs.ap[:, 0:n], scalar=gvec.ap, in1=rd.ap[:, 0:n],
                                                             op0=ALU.mult, op1=ALU.mult), [xs, gvec, rd], [dst_t])

            def qk_b(u):
                kind, idx, b = units[u]
                t0, n, s_, tl = blocks[b]
                if s_ != 0:
                    return
                r_ = u % NR
                bC = P.bank[5 + u % 2]
                xn, at, bt = xns[u % NX], ats[r_], bts[r_]
                dst, dst_t, gvec = dest_of(kind, idx, t0, t0 + n)
                P.op(pe, lambda: PE_.matmul(bC.ap[:, 0:n], lhsT=rotmb.ap, rhs=xn.ap[:, 0:n], start=True, stop=True), [rotmb, xn], [bC])
                P.op(pool, lambda: G_.tensor_tensor(out=at.ap[:, 0:n], in0=xn.ap[:, 0:n], in1=rcos.ap[:, tl:tl + n], op=ALU.mult), [xn, rcos], [at])
                P.op(dve, lambda: V.tensor_tensor(out=bt.ap[:, 0:n], in0=bC.ap[:, 0:n], in1=rsin.ap[:, tl:tl + n], op=ALU.mult), [bC, rsin], [bt])
                P.op(dve, lambda: V.tensor_tensor(out=dst, in0=at.ap[:, 0:n], in1=bt.ap[:, 0:n], op=ALU.add), [at, bt], [dst_t])

            nu = len(units)
            for u in range(nu + 4):
                if u < nu:
                    qk_p(u)
                if 0 <= u - 1 < nu:
                    qk_a(u - 1)
                if 0 <= u - 2 < nu:
                    qk_a2(u - 2)
                if 0 <= u - 4 < nu:
                    qk_b(u - 4)
            wv = load_w(wis, V_OFF)
            P.op(pool, lambda: G_.memset(vtok.ap[:, :, :, 64:128], 1.0), [], [vtok])
            for i, (j, s_) in enumerate(tiles):
                bank = P.bank[(i % 2) * 4 + 3]
                bi = blk_index(blocks, j * 128)
                for dc in range(DC):
                    P.op(pe, lambda: PE_.matmul(bank.ap[:, 0:128], lhsT=uT.ap[:, dc, j * 128:(j + 1) * 128], rhs=wv.ap[:, dc, :],
                                                start=(dc == 0), stop=(dc == DC - 1)), [ublk[bi], wv], [bank], inc=(dc == DC - 1))
                P.op(dve, lambda: V.tensor_copy(out=vtok.ap[:, j, :, 0:64], in_=bank.ap[:, 0:128].rearrange("p (h d) -> p h d", h=2)), [bank], [vtok])
            P.barrier()
            if stop == "m2c":
                P.pbot = pm
                return
            P.pbot = mark_u
            mix_att = P.palloc("mix_att", [4, T], BF16)
            mix_hy = P.palloc("mix_hy", [2, T], BF16)

            LA = 4
            pts = [P.talloc(f"pt{i}", [512], BF16) for i in range(6)]
            mod_ems = mod_emitters(l + 1, lambda k: P.bank[5]) if l + 1 < DEPTH else []
            recs = [P.talloc(f"rec{i}", [512], F32) for i in range(2)]
            cnts = 0
            cnto = 0
            for qs in mstreams:
                keytiles = list(range(TT)) if qs == 0 else list(range(LT, TT))
                qblocks = [b for b in blocks if b[2] == qs]
                for h in range(8):
                    kh, qi = h // 4, h // 2
                    pr = slice((h % 2) * 64, (h % 2) * 64 + 64)
                    for (t0, n, s_, tl) in qblocks:
                        bo = P.bank[6 + cnto % 2]
                        rec = recs[cnto % 2]
                        cnto += 1
                        nk = len(keytiles)
                        slots = {}

                        def emit_qk(ii):
                            nonlocal cnts
                            i_ = keytiles[ii]
                            bs, pt = P.bank[cnts % 5], pts[cnts % 6]
                            cnts += 1
                            slots[ii] = pt
                            P.op(pe, lambda: PE_.matmul(bs.ap[:, 0:n], lhsT=kdup.ap[pr, kh, i_ * 128:(i_ + 1) * 128], rhs=qT.ap[pr, qi, t0:t0 + n],
                                                        start=True, stop=True), [kdup, qT], [bs])
                            P.op(act, lambda: S_.activation(out=pt.ap[:, 0:n], in_=bs.ap[:, 0:n], func=AF.Exp, scale=ATTN_SCALE), [bs], [pt])

                        def emit_pv(ii):
                            i_ = keytiles[ii]
                            pt = slots.pop(ii)
                            P.op(pe, lambda: PE_.matmul(bo.ap[:, 0:n], lhsT=vtok.ap[:, i_, kh, :], rhs=pt.ap[:, 0:n],
                                                        start=(ii == 0), stop=(ii == nk - 1)), [vtok, pt], [bo], inc=(ii == nk - 1))

                        for ii in range(nk + LA):
                            if ii < nk:
                                emit_qk(ii)
                            if ii >= LA:
                                emit_pv(ii - LA)
                        P.op(dve, lambda: V.reciprocal(out=rec.ap[64:128, 0:n], in_=bo.ap[64:128, 0:n]), [bo], [rec])
                        P.op(dve, lambda: V.tensor_tensor(out=mix_att.ap[pr, qi, t0:t0 + n], in0=bo.ap[0:64, 0:n], in1=rec.ap[64:128, 0:n], op=ALU.mult),
                             [bo, rec], [mix_att])
                        if mod_ems:
                            mod_ems.pop(0)()
            while mod_ems:
                mod_ems.pop(0)()
            P.barrier()
            if stop == "m3":
                P.pbot = pm
                return

            hbias = P.palloc("hbias", [2, 2], F32)
            P.dma(sp, [(hbias.ap[:, o, :], hy_bias[l, o, :].rearrange("(cc p) -> p cc", p=128)) for o in range(2)], [dmisc], [hbias], hbias)
            for s_ in mstreams:
                Lx = L if s_ == 0 else C
                g0 = 0 if s_ == 0 else L
                tag = "L" if s_ == 0 else "C"
                nt = Lx // 128
                Fh, Gh, kf = cst["Fh_" + tag], cst["Gh_" + tag], Kf[(l, s_)]
                ztok = P.talloc(f"ztok{s_}", [nt, 256], BF16)
                z2f = P.reg(Tile(f"z2f{s_}", mix_hy.ap[:, :, g0:g0 + Lx]))
                Y = P.talloc(f"Y{s_}", [2 * nt, 256], BF16)
                Fsl = [[P.talloc(f"hF{s_}{i}_{k}", [nt * 128], BF16) for k in range(2)] for i in range(2)]
                Ksl = [[P.talloc(f"hK{s_}{i}_{k}", [256], F32) for k in range(2)] for i in range(2)]
                GR = 4
                NGS = 4
                Gsl = [P.talloc(f"hG{s_}{i}", [GR, 512], BF16) for i in range(NGS)]
                tq = [[P.talloc(f"hq{s_}{i}_{k}", [256], F32) for k in range(4)] for i in range(2)]
                tfs = [P.talloc(f"htf{s_}{i}", [512], F32) for i in range(2)]
                sblocks = [b for b in blocks if b[2] == s_]
                cx = 0
                cg = 0
                ce = 0
                for o in range(2):
                    zin = (lambda cc: zc[s_].ap[:, cc, :]) if o == 0 else (lambda cc: z2f.ap[:, cc, :])
                    zin_t = zc[s_] if o == 0 else z2f
                    gate = lambda cc: zc[s_].ap[:, 2 + 2 * o + cc, :]
                    for tc in range(nt):
                        bank = P.bank[cx % 2]
                        bview = bank.ap.bitcast(BF16)
                        for cc in range(2):
                            P.op(pe, lambda: PE_.transpose(out=bview[:, cc * 128:(cc + 1) * 128], in_=zin(cc)[:, tc * 128:(tc + 1) * 128],
                                                           identity=identb.ap), [zin_t, identb], [bank], inc=(cc == 1))
                        if cx % 2 == 0:
                            P.op(dve, lambda: V.tensor_copy(out=ztok.ap[:, tc, :], in_=bview[:, 0:256]), [bank], [ztok])
                        else:
                            P.op(act, lambda: S_.copy(out=ztok.ap[:, tc, :], in_=bview[:, 0:256]), [bank], [ztok])
                        cx += 1
                    for fc in range(nt):
                        Fc_, Fs_ = Fsl[fc % 2]
                        Kr, Ki = Ksl[fc % 2]
                        P.dma(sp, [(Fc_.ap, Fh[fc])], [dmisc], [Fc_], Fc_)
                        P.dma(sp, [(Fs_.ap, Fh[nt + fc])], [dmisc], [Fs_], Fs_)
                        P.dma(sp, [(Kr.ap, kf[fc * 128:(fc + 1) * 128, o * 256:(o + 1) * 256])], [Kf_t[(l, s_)]], [Kr], Kr)
                        P.dma(sp, [(Ki.ap, kf[(nt + fc) * 128:(nt + fc + 1) * 128, o * 256:(o + 1) * 256])], [Kf_t[(l, s_)]], [Ki], Ki)
                        bUc, bUs = P.bank[2 + 2 * (fc % 2)], P.bank[3 + 2 * (fc % 2)]
                        for sc in range(nt):
                            P.op(pe, lambda: PE_.matmul(bUc.ap[:, 0:256], lhsT=Fc_.ap[:, sc * 128:(sc + 1) * 128], rhs=ztok.ap[:, sc, :],
                                                        start=(sc == 0), stop=(sc == nt - 1)), [Fc_, ztok], [bUc], inc=(sc == nt - 1))
                        for sc in range(nt):
                            P.op(pe, lambda: PE_.matmul(bUs.ap[:, 0:256], lhsT=Fs_.ap[:, sc * 128:(sc + 1) * 128], rhs=ztok.ap[:, sc, :],
                                                        start=(sc == 0), stop=(sc == nt - 1)), [Fs_, ztok], [bUs], inc=(sc == nt - 1))
                        q1, q2, q3, q4 = tq[fc % 2]
                        P.op(dve, lambda: V.tensor_tensor(out=q1.ap, in0=bUc.ap[:, 0:256], in1=Kr.ap, op=ALU.mult), [bUc, Kr], [q1])
                        P.op(dve, lambda: V.tensor_tensor(out=q2.ap, in0=bUs.ap[:, 0:256], in1=Ki.ap, op=ALU.mult), [bUs, Ki], [q2])
                        P.op(pool, lambda: G_.tensor_tensor(out=Y.ap[:, fc, :], in0=q1.ap, in1=q2.ap, op=ALU.subtract), [q1, q2], [Y])
                        P.op(dve, lambda: V.tensor_tensor(out=q3.ap, in0=bUc.ap[:, 0:256], in1=Ki.ap, op=ALU.mult), [bUc, Ki], [q3])
                        P.op(dve, lambda: V.tensor_tensor(out=q4.ap, in0=bUs.ap[:, 0:256], in1=Kr.ap, op=ALU.mult), [bUs, Kr], [q4])
                        P.op(pool, lambda: G_.tensor_tensor(out=Y.ap[:, nt + fc, :], in0=q3.ap, in1=q4.ap, op=ALU.add), [q3, q4], [Y])
                    Gv = Gh.rearrange("(r p) t -> p r t", p=128)
                    for (t0, n, _s, tl) in sblocks:
                        bO = [P.bank[6], P.bank[7]]
                        rc = 0
                        while rc < 2 * nt:
                            ng = min(GR, 2 * nt - rc)
                            gs = Gsl[cg % NGS]
                            cg += 1
                            P.dma(sp, [(gs.ap[:, 0:ng, 0:n], Gv[:, rc:rc + ng, tl:tl + n])], [dmisc], [gs], gs)
                            for r in range(ng):
                                for cc in range(2):
                                    P.op(pe, lambda: PE_.matmul(bO[cc].ap[:, 0:n], lhsT=Y.ap[:, rc + r, cc * 128:(cc + 1) * 128], rhs=gs.ap[:, r, 0:n],
                                                                start=(rc + r == 0), stop=(rc + r == 2 * nt - 1)), [Y, gs], [bO[cc]],
                                         inc=(rc + r == 2 * nt - 1) or (r == ng - 1 and cc == 1))
                            rc += ng
                        for cc in range(2):
                            tf = tfs[ce % 2]
                            ce += 1
                            P.op(dve, lambda: V.scalar_tensor_tensor(out=tf.ap[:, 0:n], in0=zin(cc)[:, tl:tl + n], scalar=hbias.ap[:, o, cc:cc + 1],
                                                                     in1=bO[cc].ap[:, 0:n], op0=ALU.mult, op1=ALU.add), [zin_t, hbias, bO[cc]], [tf])
                            if o == 0:
                                P.op(pool, lambda: G_.tensor_tensor(out=z2f.ap[:, cc, tl:tl + n], in0=tf.ap[:, 0:n], in1=gate(cc)[:, tl:tl + n], op=ALU.mult),
                                     [tf, zc[s_]], [z2f])
                            else:
                                P.op(pool, lambda: G_.tensor_tensor(out=mix_hy.ap[:, cc, g0 + tl:g0 + tl + n], in0=tf.ap[:, 0:n], in1=gate(cc)[:, tl:tl + n],
                                                                    op=ALU.mult), [tf, zc[s_]], [mix_hy, z2f])
                P.barrier()
            if stop == "m4":
                P.pbot = pm
                return

            Wo = P.talloc("Wo", [8, D], BF16)
            P.dma(pool, [(Wo.ap[:, mc, :], w_out[l][mc * 128:(mc + 1) * 128, :]) for mc in range(8)], [dmisc], [Wo], Wo)
            NHB = 4
            hbs = [P.talloc(f"oh{i}", [D], F32) for i in range(NHB)]
            tts = [P.talloc(f"ott{i}", [512], F32) for i in range(2)]
            otiles = [(j, s_) for (j, s_) in tiles if s_ in mstreams]
            mixsrc = lambda mc, a, b_: (mix_pool.ap[:, mc, a:b_] if mc < 2 else (mix_hy.ap[:, mc - 2, a:b_] if mc < 4 else mix_att.ap[:, mc - 4, a:b_]))
            nhalf = (D + 511) // 512
            cnto = 0
            loaded = 0
            for i, (j, s_) in enumerate(otiles):
                while loaded <= min(i + 2, len(otiles) - 1):
                    jj = otiles[loaded][0]
                    P.dma(sp, [(hbs[loaded % NHB].ap, Hd[jj].ap)], [Hd[jj]], [hbs[loaded % NHB]], hbs[loaded % NHB])
                    loaded += 1
                hb = hbs[i % NHB]
                for hf in range(nhalf):
                    n = min(512, D - hf * 512)
                    bo = P.bank[cnto % 4]
                    tt = tts[cnto % 2]
                    cnto += 1
                    for mc in range(8):
                        P.op(pe, lambda: PE_.matmul(bo.ap[:, 0:n], lhsT=mixsrc(mc, j * 128, (j + 1) * 128), rhs=Wo.ap[:, mc, hf * 512:hf * 512 + n],
                                                    start=(mc == 0), stop=(mc == 7)), [mix_pool, mix_hy, mix_att, Wo], [bo], inc=(mc == 7))
                    P.op(dve, lambda: V.tensor_tensor(out=tt.ap[:, 0:n], in0=bo.ap[:, 0:n], in1=Cc[s_].ap[:, hf * 512:hf * 512 + n], op=ALU.mult),
                         [bo, Cc[s_]], [tt])
                    P.op(pool, lambda: G_.tensor_tensor(out=hb.ap[:, hf * 512:hf * 512 + n], in0=hb.ap[:, hf * 512:hf * 512 + n], in1=tt.ap[:, 0:n],
                                                        op=ALU.add), [hb, tt], [hb])
                P.dma(sp, [(Hd[j].ap, hb.ap)], [hb], [Hd[j]], hb)
            P.barrier()
            P.pbot = pm

        stop = getattr(cfg, "stop", None)
        P.mute = getattr(cfg, "mute", None)

        def run_all():
            P.barrier()
            if stop == "init":
                return
            phase_mod()
            if stop == "mod":
                return
            for l in range(DEPTH):
                phase_filters(l, [0] if l == DEPTH - 1 else [0, 1])
                if stop == "filt":
                    return
            if stop == "filtall":
                return
            for l in range(DEPTH):
                sublayer_ffn(l, 0)
                if stop in ("ffn", "ffnnorm"):
                    return
                sublayer_mixer(l)
                if stop in ("mix", "m1", "m2a", "m2b", "m2c", "m3", "m4"):
                    return
                sublayer_ffn(l, 1)

        run_all()
        P.barrier()
    return nc


_WNAMES = ["norm_g", "w_mod", "b_mod", "ffn_w_gate", "ffn_w_up", "ffn_w_down", "w_in", "w_out", "pool_w", "pool_scale",
           "hyena_conv_w", "hyena_conv_b", "hyena_f_w1", "hyena_f_b1", "hyena_f_w2", "hyena_f_b2", "hyena_f_w3",
           "hyena_sin_freq", "hyena_bias", "q_norm_g", "k_norm_g"]


def make_in_maps(cfg, consts, inputs, B):
    f = lambda a: np.ascontiguousarray(np.asarray(a, dtype=np.float32))
    shared = {n: f(inputs[n]) for n in _WNAMES}
    for k, v in consts.items():
        shared["k_" + k] = v
    xs, cs, ctxs = f(inputs["x"]), f(inputs["c"]), f(inputs["ctx"])
    cctx = f(inputs["c_ctx"]).reshape(1, cfg.D)
    maps = []
    for b in range(B):
        m = dict(shared)
        m["x"] = np.ascontiguousarray(xs[b])
        m["ctx"] = np.ascontiguousarray(ctxs[b])
        m["c"] = np.ascontiguousarray(cs[b:b + 1])
        m["c_ctx"] = cctx
        maps.append(m)
    return maps


def kernel(**inputs):
    cfg = Cfg()
    consts = host_constants(cfg)
    nc = build_program(cfg, consts)
    B = 8
    maps = make_in_maps(cfg, consts, inputs, B)
    res = run_bass_kernel_spmd(nc, maps, core_ids=list(range(B)))
    return np.stack([np.asarray(r["out"], dtype=np.float32) for r in res.results], axis=0)
```

```python
import math
from contextlib import ExitStack
import numpy as np
import ml_dtypes
import concourse.bass as bass
import concourse.mybir as mybir
from concourse.bass_utils import run_bass_kernel_spmd

F32 = mybir.dt.float32
BF16 = mybir.dt.bfloat16
ALU = mybir.AluOpType
AF = mybir.ActivationFunctionType

EPS = 1e-6
POOL_WINDOWS = (2, 4, 8, 16)
HY_OFF, Q_OFF, K_OFF, V_OFF, IN_W = 256, 1024, 1536, 1664, 1792
GRID_W = 64
ATTN_SCALE = 64 ** -0.5
PI_S = 3.1415925
TWO_PI = 2.0 * math.pi
SB_WORDS = 49152


class Cfg:
    def __init__(s, D=1024, L=2048, C=256, DFF=2816, DEPTH=4, NG=2):
        s.D, s.L, s.C, s.DFF, s.DEPTH, s.NG = D, L, C, DFF, DEPTH, NG
        s.DC = D // 128
        s.LT = L // 128
        s.CT = C // 128
        s.T = L + C
        s.TT = s.T // 128
        s.FC = DFF // 128


class Sem:
    def __init__(s, h):
        s.h = h
        s.total = 0


class Ev:
    __slots__ = ("sem", "val")

    def __init__(s, sem, val):
        s.sem = sem
        s.val = val


class Tile:
    def __init__(s, name, ap):
        s.name = name
        s.ap = ap
        s.w = None
        s.r = {}
        s.dsem = {}


class Eng:
    def __init__(s, name, h, sem):
        s.name, s.h, s.sem = name, h, sem
        s.waited = {}
        s.pending = []


def _reshape(ap, free_shape):
    if len(free_shape) <= 1:
        return ap
    names = [f"a{i}" for i in range(len(free_shape))]
    pat = "p (" + " ".join(names) + ") -> p " + " ".join(names)
    kw = {n: int(v) for n, v in zip(names[:-1], free_shape[:-1])}
    return ap.rearrange(pat, **kw)


class Prog:
    def __init__(s, nc, stack, nds=88):
        s.nc = nc

        def mk(name):
            return Sem(stack.enter_context(nc.semaphore(name)))

        s.pe = Eng("pe", nc.tensor, mk("s_pe"))
        s.act = Eng("act", nc.scalar, mk("s_act"))
        s.dve = Eng("dve", nc.vector, mk("s_dve"))
        s.pool = Eng("pool", nc.gpsimd, mk("s_pool"))
        s.sp = Eng("sp", nc.sync, mk("s_sp"))
        s.engs = [s.pe, s.act, s.dve, s.pool, s.sp]
        s.bar = mk("s_bar")
        s.free_dsems = {"hw": [mk(f"d{i}") for i in range(nds - 24)], "sw": [mk(f"w{i}") for i in range(24)]}
        s.owners = []
        s.tiles = []
        s.sb = stack.enter_context(nc.sbuf_tensor("sbpool", [128, SB_WORDS], F32))
        s.ps = stack.enter_context(nc.psum_tensor("pspool", [128, 4096], F32))
        s.pbot = 0
        s.ttop = SB_WORDS
        s.cur_phase = None
        s.phase_count = 0
        s.mute = None
        s.bank = [s.reg(Tile(f"bank{i}", s.ps[:, i * 512:(i + 1) * 512])) for i in range(8)]

    def reg(s, t):
        s.tiles.append(t)
        return t

    def dram(s, name, ap):
        return s.reg(Tile(name, ap))

    def _carve(s, name, off, free_shape, dtype):
        n = int(np.prod(free_shape))
        words = n if dtype == F32 else (n + 1) // 2
        ap = s.sb[:, off:off + words]
        if dtype != F32:
            ap = ap.bitcast(dtype)
            if 2 * words != n:
                ap = ap[:, 0:n]
        return s.reg(Tile(name, _reshape(ap, list(free_shape))))

    @staticmethod
    def _words(free_shape, dtype):
        n = int(np.prod(free_shape))
        return n if dtype == F32 else (n + 1) // 2

    def palloc(s, name, free_shape, dtype):
        w = s._words(free_shape, dtype)
        off = s.pbot
        s.pbot += w
        assert s.pbot <= s.ttop, f"SBUF overflow (persist) at {name}: {s.pbot} > {s.ttop}"
        return s._carve(name, off, free_shape, dtype)

    def talloc(s, name, free_shape, dtype):
        w = s._words(free_shape, dtype)
        s.ttop -= w
        assert s.pbot <= s.ttop, f"SBUF overflow (transient) at {name}: {s.pbot} > {s.ttop}"
        return s._carve(name, s.ttop, free_shape, dtype)

    def need(s, eng, ev):
        if ev is None:
            return
        if eng is s.pe and ev.sem is s.pe.sem:
            return
        assert ev.val is not None, "unresolved PE event"
        k = id(ev.sem)
        if eng.waited.get(k, 0) >= ev.val:
            return
        eng.h.wait_ge(ev.sem.h, ev.val)
        eng.waited[k] = ev.val

    def deps(s, eng, reads, writes):
        for t in reads:
            s.need(eng, t.w)
        for t in writes:
            s.need(eng, t.w)
            for e in list(t.r.values()):
                s.need(eng, e)

    def mark(s, ev, reads, writes):
        for t in reads:
            t.r[id(ev.sem)] = ev
        for t in writes:
            t.w = ev
            t.r = {}

    def phase(s, name):
        s.cur_phase = name
        s.phase_count = 0

    def _muted(s, eng):
        if s.mute is not None and s.mute[0] == s.cur_phase:
            if s.phase_count >= s.mute[1] and not (eng is s.pe and s.pe.pending):
                return True
            s.phase_count += 1
        return False

    def op(s, eng, fn, reads=(), writes=(), inc=True):
        if s._muted(eng):
            return
        s.deps(eng, reads, writes)
        ins = fn()
        if inc:
            eng.sem.total += 1
            ins.then_inc(eng.sem.h, 1)
            ev = Ev(eng.sem, eng.sem.total)
            for p in eng.pending:
                p.val = eng.sem.total
            eng.pending = []
        else:
            ev = Ev(eng.sem, None)
            eng.pending.append(ev)
        s.mark(ev, reads, writes)

    def dma(s, q, pairs, reads, writes, owner):
        if s._muted(q):
            return
        s.deps(q, reads, writes)
        kind = "sw" if q is s.pool else "hw"
        if kind not in owner.dsem:
            assert s.free_dsems[kind], "out of DMA semaphores"
            if not owner.dsem:
                s.owners.append(owner)
            owner.dsem[kind] = s.free_dsems[kind].pop()
        sem = owner.dsem[kind]
        if sem.total > 0:
            s.need(q, Ev(sem, sem.total))
        for (o, i) in pairs:
            q.h.dma_start(out=o, in_=i).then_inc(sem.h, 16)
            sem.total += 16
        s.mark(Ev(sem, sem.total), reads, writes)

    def barrier(s):
        sp = s.sp
        for e in s.engs:
            assert not e.pending
            if e is not sp and e.sem.total > 0:
                s.need(sp, Ev(e.sem, e.sem.total))
        for o in s.owners:
            for kind, sem in o.dsem.items():
                s.need(sp, Ev(sem, sem.total))
                s.free_dsems[kind].append(sem)
            o.dsem = {}
        s.owners = []
        s.bar.total += 1
        sp.h.sem_inc(s.bar.h, 1)
        for e in s.engs:
            if e is not sp:
                e.h.wait_ge(s.bar.h, s.bar.total)
        for t in s.tiles:
            t.w = None
            t.r = {}
        s.ttop = SB_WORDS


def _bf(a):
    return np.ascontiguousarray(a.astype(np.float32)).astype(ml_dtypes.bfloat16)


def host_constants(cfg):
    c = {}
    c["identb"] = _bf(np.eye(128))
    c["onesf"] = np.ones((128, 128), np.float32)
    bo = np.zeros((128, 128), np.float32)
    bo[:64, :64] = 1.0
    bo[64:, 64:] = 1.0
    c["blockones"] = bo
    R = np.zeros((64, 64), np.float32)
    for dp in range(64):
        q = dp // 16
        if q in (0, 2):
            R[dp + 16, dp] = -1.0
        else:
            R[dp - 16, dp] = 1.0
    rm = np.zeros((128, 128), np.float32)
    rm[:64, :64] = R
    rm[64:, 64:] = R
    c["rotm"] = rm
    c["blockonesb"] = _bf(bo)
    c["rotmb"] = _bf(rm)
    L = cfg.L
    rows = L // GRID_W
    row = np.repeat(np.arange(rows), GRID_W).astype(np.float32)
    col = np.tile(np.arange(GRID_W), rows).astype(np.float32)
    inv_freq = (1.0 / (np.float32(10000.0) ** (np.arange(0, 32, 2, dtype=np.float32) / np.float32(32)))).astype(np.float32)
    ang_r = row[:, None] * inv_freq
    ang_c = col[:, None] * inv_freq
    ang = np.concatenate([ang_r, ang_r, ang_c, ang_c], axis=-1).astype(np.float32)
    cosT = np.cos(ang).T.astype(np.float32)
    sinT = np.sin(ang).T.astype(np.float32)
    c["ropecos"] = np.ascontiguousarray(np.concatenate([cosT, cosT], 0))
    c["ropesin"] = np.ascontiguousarray(np.concatenate([sinT, sinT], 0))
    invw = np.zeros((128, 2), np.float32)
    invc = np.zeros((128, 2, 16), np.float32)
    for ci in range(2):
        for p in range(128):
            w = POOL_WINDOWS[2 * ci + p // 64]
            invw[p, ci] = 1.0 / w
            for e in range(16):
                if e < 8:
                    t = e
                    lo = max(t - w // 2, 0)
                    hi = t - w // 2 + w
                else:
                    d = 8 - (e - 8)
                    lo = -d - w // 2
                    hi = min(-d - w // 2 + w, 0)
                invc[p, ci, e] = 1.0 / float(hi - lo)
    c["invw"] = invw
    c["invc"] = invc.reshape(128, 32)
    for tag, Lx in (("L", cfg.L), ("C", cfg.C)):
        f32 = np.float32
        t = np.linspace(0.0, 1.0, Lx, dtype=f32)[:, None]
        bands = 16
        w_ang = (f32(2.0 * math.pi) * np.arange(Lx, dtype=f32) / f32(Lx)).astype(f32)
        freqs = np.linspace(1e-4, bands - 1, bands, dtype=f32)
        ang = (w_ang[:, None] * freqs[None, :]).astype(f32)
        z = np.concatenate([t, np.cos(ang), -np.sin(ang)], axis=-1).astype(f32)
        c["zT_" + tag] = np.ascontiguousarray(z.T)
        max_decay = math.log(1e-2) / 0.3
        min_decay = math.log(1e-2) / 1.5
        deltas = np.linspace(min_decay, max_decay, 256, dtype=f32)
        c["decay_" + tag] = np.exp(-t * np.abs(deltas)).astype(f32)
        N = 2 * Lx
        nt = Lx // 128
        s_idx = np.arange(Lx, dtype=np.float64)[:, None]
        th = 2.0 * np.pi * (np.arange(Lx, dtype=np.float64) + 0.5) / N
        Fc = np.cos(s_idx * th[None, :])
        Fs = -np.sin(s_idx * th[None, :])
        Fm = np.concatenate([Fc, Fs], axis=1)
        Fh = Fm.reshape(nt, 128, 2 * nt, 128).transpose(2, 1, 0, 3).reshape(2 * nt, 128, nt * 128)
        c["Fh_" + tag] = _bf(Fh)
        c["Gh_" + tag] = _bf((2.0 / N) * Fm.T)
    return c


CONST_SHAPES = None


def build_program(cfg, consts):
    nc = bass.Bass("TRN2", target_bir_lowering=False)
    D, L, C, DFF, DEPTH = cfg.D, cfg.L, cfg.C, cfg.DFF, cfg.DEPTH
    DC, LT, CT, T, TT, FC = cfg.DC, cfg.LT, cfg.CT, cfg.T, cfg.TT, cfg.FC

    def din(name, shape, dt=F32):
        return nc.dram_tensor(name, list(shape), dt, kind="ExternalInput").ap()

    x = din("x", [L, D])
    ctx = din("ctx", [C, D])
    c_in = din("c", [1, D])
    cc_in = din("c_ctx", [1, D])
    norm_g = din("norm_g", [DEPTH, 3, D])
    w_mod = din("w_mod", [DEPTH, D, 9 * D])
    b_mod = din("b_mod", [DEPTH, 9 * D])
    w_gate = din("ffn_w_gate", [DEPTH, 2, D, DFF])
    w_up = din("ffn_w_up", [DEPTH, 2, D, DFF])
    w_down = din("ffn_w_down", [DEPTH, 2, DFF, D])
    w_in = din("w_in", [DEPTH, D, IN_W])
    w_out = din("w_out", [DEPTH, 1024, D])
    pool_w = din("pool_w", [DEPTH, 4, 64, 64])
    pool_scale = din("pool_scale", [DEPTH, 256])
    hconv_w = din("hyena_conv_w", [DEPTH, 3, 768])
    hconv_b = din("hyena_conv_b", [DEPTH, 768])
    f_w1 = din("hyena_f_w1", [DEPTH, 33, 64])
    f_b1 = din("hyena_f_b1", [DEPTH, 64])
    f_w2 = din("hyena_f_w2", [DEPTH, 64, 64])
    f_b2 = din("hyena_f_b2", [DEPTH, 64])
    f_w3 = din("hyena_f_w3", [DEPTH, 64, 1024])
    sin_freq = din("hyena_sin_freq", [DEPTH, 2, 64])
    hy_bias = din("hyena_bias", [DEPTH, 2, 256])
    q_norm_g = din("q_norm_g", [DEPTH, 64])
    k_norm_g = din("k_norm_g", [DEPTH, 64])
    cst = {}
    for k, v in consts.items():
        cst[k] = din("k_" + k, v.shape, BF16 if v.dtype == ml_dtypes.bfloat16 else F32)
    out = nc.dram_tensor("out", [L, D], F32, kind="ExternalOutput").ap()
    Hs = nc.dram_tensor("Hs", [T, D], F32, kind="Internal").ap()
    modv = nc.dram_tensor("modv", [DEPTH, 2, 9 * D], F32, kind="Internal").ap()
    Kf = {}
    for l in range(DEPTH):
        for s_, Lx in ((0, L), (1, C)):
            Kf[(l, s_)] = nc.dram_tensor(f"Kf{l}_{s_}", [2 * Lx, 512], F32, kind="Internal").ap()

    stack = ExitStack()
    with stack:
        stack.enter_context(nc.allow_non_contiguous_dma(reason="small strided parameter loads"))
        P = Prog(nc, stack)
        pe, act, dve, pool, sp = P.pe, P.act, P.dve, P.pool, P.sp
        V, S_, G_, PE_ = nc.vector, nc.scalar, nc.gpsimd, nc.tensor

        Hd = [P.dram(f"H{j}", Hs[j * 128:(j + 1) * 128, :]) for j in range(TT)]
        Xd = [P.dram(f"X{j}", (x[j * 128:(j + 1) * 128, :] if j < LT else ctx[(j - LT) * 128:(j - LT + 1) * 128, :]))
              for j in range(TT)]
        Od = [P.dram(f"O{j}", out[j * 128:(j + 1) * 128, :]) for j in range(LT)]
        dmisc = P.dram("dmisc", Hs)
        modv_t = [P.dram(f"modv{l}", modv[l]) for l in range(DEPTH)]
        Kf_t = {k: P.dram(f"Kft{k}", v) for k, v in Kf.items()}

        identb = P.palloc("identb", [128], BF16)
        onesf = P.palloc("onesf", [128], F32)
        blockones = P.palloc("blockones", [128], F32)
        rotm = P.palloc("rotm", [128], F32)
        blockonesb = P.palloc("blockonesb", [128], BF16)
        rotmb = P.palloc("rotmb", [128], BF16)
        epsc = P.palloc("epsc", [1], F32)
        cact = P.palloc("cact", [DC, 2], BF16)
        onesb = P.palloc("onesb", [128], BF16)
        invw = P.palloc("invw", [2], F32)
        invc = P.palloc("invc", [2, 16], F32)
        neg_half = None
        P.dma(sp, [(identb.ap, cst["identb"])], [dmisc], [identb], identb)
        P.dma(sp, [(onesf.ap, cst["onesf"])], [dmisc], [onesf], onesf)
        P.dma(sp, [(blockones.ap, cst["blockones"])], [dmisc], [blockones], blockones)
        P.dma(sp, [(rotm.ap, cst["rotm"])], [dmisc], [rotm], rotm)
        P.dma(sp, [(invw.ap, cst["invw"])], [dmisc], [invw], invw)
        P.dma(sp, [(blockonesb.ap, cst["blockonesb"])], [dmisc], [blockonesb], blockonesb)
        P.dma(sp, [(rotmb.ap, cst["rotmb"])], [dmisc], [rotmb], rotmb)
        P.op(dve, lambda: V.memset(epsc.ap, EPS), [], [epsc])
        P.op(dve, lambda: V.memset(onesb.ap, 1.0), [], [onesb])
        P.dma(sp, [(invc.ap, cst["invc"].rearrange("p (a b) -> p a b", a=2))], [dmisc], [invc], invc)
        base_mark = None

        def blocks_of(streams):
            res = []
            for s_ in streams:
                g0, Lx = (0, L) if s_ == 0 else (L, C)
                t = 0
                while t < Lx:
                    n = min(512, Lx - t)
                    res.append((g0 + t, n, s_, t))
                    t += n
            return res

        def blk_index(blocks, tglob):
            for i, (t0, n, s_, tl) in enumerate(blocks):
                if t0 <= tglob < t0 + n:
                    return i
            raise AssertionError

        def mod_setup():
            ca = P.talloc("ca", [DC], F32)
            cb = P.talloc("cb", [DC], F32)
            P.dma(sp, [(ca.ap, c_in[0, :].rearrange("(dc p) -> p dc", p=128))], [dmisc], [ca], ca)
            P.dma(sp, [(cb.ap, cc_in[0, :].rearrange("(dc p) -> p dc", p=128))], [dmisc], [cb], cb)
            P.op(act, lambda: S_.activation(out=cact.ap[:, :, 0], in_=ca.ap, func=AF.Silu), [ca], [cact])
            P.op(act, lambda: S_.activation(out=cact.ap[:, :, 1], in_=cb.ap, func=AF.Silu), [cb], [cact])

        def mod_emitters(l, bank_fn):
            J = 9 * D
            wms = [P.talloc(f"wm{i}", [DC, 512], BF16) for i in range(3)]
            bmt = [P.talloc(f"bm{i}", [512], F32) for i in range(2)]
            mvt = [P.talloc(f"mv{i}", [512], F32) for i in range(2)]
            wsrc = w_mod[l].rearrange("(dc p) j -> p dc j", p=128)
            blks = []
            j0 = 0
            while j0 < J:
                n = min(512, J - j0)
                blks.append((j0, n))
                j0 += n
            issued = [0]

            def prefetch(upto):
                while issued[0] <= min(upto, len(blks) - 1):
                    kk = issued[0]
                    jj, nn = blks[kk]
                    slot = wms[kk % 3]
                    P.dma(pool, [(slot.ap[:, :, 0:nn], wsrc[:, :, jj:jj + nn])], [dmisc], [slot], slot)
                    issued[0] += 1

            ems = []
            for k, (j0, n) in enumerate(blks):

                def emit(j0=j0, n=n, k=k):
                    slot, bm, mv, bank = wms[k % 3], bmt[k % 2], mvt[k % 2], bank_fn(k)
                    prefetch(k + 1)
                    P.dma(sp, [(bm.ap[0:2, 0:n], b_mod[l:l + 1, j0:j0 + n].to_broadcast([2, n]))], [dmisc], [bm], bm)
                    for dc in range(DC):
                        P.op(pe, lambda: PE_.matmul(bank.ap[0:2, 0:n], lhsT=cact.ap[:, dc, :], rhs=slot.ap[:, dc, 0:n],
                                                    start=(dc == 0), stop=(dc == DC - 1)),
                             [cact, slot], [bank], inc=(dc == DC - 1))
                    P.op(dve, lambda: V.tensor_tensor(out=mv.ap[0:2, 0:n], in0=bank.ap[0:2, 0:n], in1=bm.ap[0:2, 0:n], op=ALU.add),
                         [bank, bm], [mv])
                    P.dma(sp, [(modv[l][:, j0:j0 + n], mv.ap[0:2, 0:n])], [mv], [modv_t[l]], mv)
                    prefetch(k + 2)

                ems.append(emit)
            prefetch(1)
            return ems

        def phase_mod():
            mod_setup()
            for e in mod_emitters(0, lambda k: P.bank[k % 4]):
                e()
            P.barrier()

        def phase_filters(l, s_):
            Lx = L if s_ == 0 else C
            tag = "L" if s_ == 0 else "C"
            nt = Lx // 128
            pm_f = P.pbot
            h2 = P.palloc("h2", [Lx], F32)
            w1 = P.talloc("fw1", [64], F32)
            w2 = P.talloc("fw2", [64], F32)
            vec = P.talloc("fvec", [8], F32)
            zT = P.talloc("zT", [Lx], F32)
            P.dma(sp, [(w1.ap[0:33, :], f_w1[l])], [dmisc], [w1], w1)
            P.dma(sp, [(w2.ap[0:64, :], f_w2[l])], [dmisc], [w2], w2)
            P.dma(sp, [(vec.ap[0:64, 0:1], f_b1[l, :].rearrange("(p o) -> p o", o=1)),
                       (vec.ap[0:64, 1:2], sin_freq[l, 0, :].rearrange("(p o) -> p o", o=1)),
                       (vec.ap[0:64, 2:3], f_b2[l, :].rearrange("(p o) -> p o", o=1)),
                       (vec.ap[0:64, 3:4], sin_freq[l, 1, :].rearrange("(p o) -> p o", o=1))], [dmisc], [vec], vec)
            P.dma(sp, [(zT.ap[0:33, :], cst["zT_" + tag])], [dmisc], [zT], zT)
            P.op(dve, lambda: V.tensor_tensor(out=vec.ap[0:64, 4:5], in0=vec.ap[0:64, 0:1], in1=vec.ap[0:64, 1:2], op=ALU.mult), [vec], [vec])
            P.op(dve, lambda: V.tensor_tensor(out=vec.ap[0:64, 5:6], in0=vec.ap[0:64, 2:3], in1=vec.ap[0:64, 3:4], op=ALU.mult), [vec], [vec])
            h1 = P.talloc("h1", [Lx], F32)
            tmps = [[P.talloc(f"ft{i}_{k}", [512], F32) for k in range(4)] for i in range(4)]
            cnt = 0

            def sin_layer(hin, K, wt, sfc, sfbc, hout):
                blks = []
                t0 = 0
                while t0 < Lx:
                    n = min(512, Lx - t0)
                    blks.append((t0, n))
                    t0 += n
                assert len(blks) <= 4
                E = list(enumerate(blks))
                for i, (t0, n) in E:
                    P.op(pe, lambda: PE_.matmul(P.bank[i].ap[0:64, 0:n], lhsT=wt.ap[0:K, 0:64], rhs=hin.ap[0:K, t0:t0 + n],
                                                start=True, stop=True), [wt, hin], [P.bank[i]])
                for i, (t0, n) in E:
                    a = tmps[i][0]
                    P.op(dve, lambda: V.tensor_scalar(out=a.ap[0:64, 0:n], in0=P.bank[i].ap[0:64, 0:n],
                                                      scalar1=vec.ap[0:64, sfc:sfc + 1], scalar2=vec.ap[0:64, sfbc:sfbc + 1],
                                                      op0=ALU.mult, op1=ALU.add), [P.bank[i], vec], [a])
                for i, (t0, n) in E:
                    a, m1, m2, a2 = tmps[i]
                    P.op(dve, lambda: V.tensor_scalar(out=m1.ap[0:64, 0:n], in0=a.ap[0:64, 0:n], scalar1=PI_S, scalar2=-TWO_PI,
                                                      op0=ALU.is_gt, op1=ALU.mult), [a], [m1])
                    P.op(dve, lambda: V.tensor_scalar(out=m2.ap[0:64, 0:n], in0=a.ap[0:64, 0:n], scalar1=-PI_S, scalar2=TWO_PI,
                                                      op0=ALU.is_lt, op1=ALU.mult), [a], [m2])
                for i, (t0, n) in E:
                    a, m1, m2, a2 = tmps[i]
                    P.op(dve, lambda: V.tensor_tensor(out=a2.ap[0:64, 0:n], in0=a.ap[0:64, 0:n], in1=m1.ap[0:64, 0:n], op=ALU.add), [a, m1], [a2])
                for i, (t0, n) in E:
                    a, m1, m2, a2 = tmps[i]
                    P.op(dve, lambda: V.tensor_tensor(out=a2.ap[0:64, 0:n], in0=a2.ap[0:64, 0:n], in1=m2.ap[0:64, 0:n], op=ALU.add), [a2, m2], [a2])
                for i, (t0, n) in E:
                    a, m1, m2, a2 = tmps[i]
                    P.op(dve, lambda: V.tensor_scalar(out=a2.ap[0:64, 0:n], in0=a2.ap[0:64, 0:n], scalar1=PI_S, scalar2=-PI_S,
                                                      op0=ALU.min, op1=ALU.max), [a2], [a2])
                for i, (t0, n) in E:
                    a, m1, m2, a2 = tmps[i]
                    P.op(act, lambda: S_.activation(out=hout.ap[0:64, t0:t0 + n], in_=a2.ap[0:64, 0:n], func=AF.Sin), [a2], [hout])

            sin_layer(zT, 33, w1, 1, 4, h1)
            sin_layer(h1, 64, w2, 3, 5, h2)
            P.barrier()
            kb = P.palloc("kb", [nt, 2, 2, 256], BF16)
            w3 = P.talloc("fw3", [1024], BF16)
            P.dma(pool, [(w3.ap[0:64, :], f_w3[l])], [dmisc], [w3], w3)
            h2b = P.talloc("h2b", [Lx], BF16)
            P.op(act, lambda: S_.copy(out=h2b.ap[0:64, :], in_=h2.ap[0:64, :]), [h2], [h2b])
            fl = P.talloc("fl", [nt, 2, 2, 256], F32)
            dec = P.talloc("dec", [nt, 256], F32)
            P.dma(sp, [(dec.ap, cst["decay_" + tag].rearrange("(tc p) c -> p tc c", p=128))], [dmisc], [dec], dec)
            for tc in range(nt):
                for o in range(2):
                    bank = P.bank[2 + (cnt % 2)]
                    cnt += 1
                    P.op(pe, lambda: PE_.matmul(bank.ap[:, 0:512], lhsT=h2b.ap[0:64, tc * 128:(tc + 1) * 128],
                                                rhs=w3.ap[0:64, o * 512:(o + 1) * 512], start=True, stop=True), [h2b, w3], [bank])
                    for d_ in range(2):
                        P.op(dve, lambda: V.tensor_tensor(out=fl.ap[:, tc, o, d_, :], in0=bank.ap[:, d_ * 256:(d_ + 1) * 256],
                                                          in1=dec.ap[:, tc, :], op=ALU.mult), [bank, dec], [fl])
            for o in range(2):
                P.op(pool, lambda: G_.memset(fl.ap[0:1, 0, o, 1, :], 0.0), [], [fl])
            absb = [P.talloc(f"absb{i}", [1024], BF16) for i in range(2)]
            cs0, cs1 = P.bank[4], P.bank[5]
            for tc in range(nt):
                ab = absb[tc % 2]
                flv = fl.ap[:, tc, :, :, :].rearrange("p o d c -> p (o d c)")
                P.op(dve, lambda: V.scalar_tensor_tensor(out=ab.ap, in0=flv, scalar=-1.0, in1=flv, op0=ALU.mult, op1=ALU.max), [fl], [ab])
                for o, bk in ((0, cs0), (1, cs1)):
                    P.op(pe, lambda: PE_.matmul(bk.ap[:, 0:512], lhsT=onesb.ap, rhs=ab.ap[:, o * 512:(o + 1) * 512],
                                                start=(tc == 0), stop=(tc == nt - 1)), [onesb, ab], [bk], inc=True)
            nrm = P.talloc("nrm", [2, 256], F32)
            rn = P.talloc("rn", [2, 256], F32)
            for o, bk in ((0, cs0), (1, cs1)):
                P.op(act, lambda: S_.copy(out=nrm.ap[:, o, :], in_=bk.ap[:, 0:256]), [bk], [nrm])
                P.op(dve, lambda: V.tensor_tensor(out=nrm.ap[:, o, :], in0=nrm.ap[:, o, :], in1=bk.ap[:, 256:512], op=ALU.add), [nrm, bk], [nrm])
            P.op(dve, lambda: V.reciprocal(out=rn.ap, in_=nrm.ap), [nrm], [rn])
            sdt = [P.talloc(f"sdt{i}", [2, 256], F32) for i in range(4)]
            for tc in range(nt):
                for j_, op_ in ((0, ALU.add), (1, ALU.subtract)):
                    P.op(dve, lambda: V.tensor_tensor(out=kb.ap[:, tc, j_, :, :], in0=fl.ap[:, tc, :, 0, :], in1=fl.ap[:, tc, :, 1, :], op=op_), [fl], [kb])
            rnp = P.palloc("rnp", [2, 256], F32)
            P.op(dve, lambda: V.tensor_copy(out=rnp.ap, in_=rn.ap), [rn], [rnp])
            P.barrier()
            Fsl = [[P.talloc(f"F{i}_{k}", [nt * 128], BF16) for k in range(2)] for i in range(2)]
            kouts = [P.talloc(f"kout{i}", [2, 2, 256], F32) for i in range(2)]
            Fh = cst["Fh_" + tag]

            def load_F(fc):
                Fc_, Fs_ = Fsl[fc % 2]
                P.dma(sp, [(Fc_.ap, Fh[fc])], [dmisc], [Fc_], Fc_)
                P.dma(sp, [(Fs_.ap, Fh[nt + fc])], [dmisc], [Fs_], Fs_)

            load_F(0)
            for fc in range(nt):
                Fc_, Fs_ = Fsl[fc % 2]
                if fc + 1 < nt:
                    load_F(fc + 1)
                bks = [P.bank[2 * (fc % 4) + i] for i in range(2)]
                for sc in range(nt):
                    for ti, Ft in enumerate((Fc_, Fs_)):
                        bk = bks[ti]
                        P.op(pe, lambda: PE_.matmul(bk.ap[:, 0:512], lhsT=Ft.ap[:, sc * 128:(sc + 1) * 128],
                                                    rhs=kb.ap[:, sc, ti, :, :].rearrange("p o c -> p (o c)"),
                                                    start=(sc == 0), stop=(sc == nt - 1)), [Ft, kb], [bk], inc=(sc == nt - 1))
                ko = kouts[fc % 2]
                rnv = rnp.ap.rearrange("p o c -> p (o c)")
                P.op(dve, lambda: V.tensor_tensor(out=ko.ap[:, 0, :, :].rearrange("p o c -> p (o c)"), in0=bks[0].ap[:, 0:512], in1=rnv, op=ALU.mult),
                     [bks[0], rnp], [ko])
                P.op(dve, lambda: V.tensor_tensor(out=ko.ap[:, 1, :, :].rearrange("p o c -> p (o c)"), in0=bks[1].ap[:, 0:512], in1=rnv, op=ALU.mult),
                     [bks[1], rnp], [ko])
                kf = Kf[(l, s_)]
                P.dma(sp, [(kf[fc * 128:(fc + 1) * 128, :], ko.ap[:, 0, :, :].rearrange("p o c -> p (o c)")),
                           (kf[(nt + fc) * 128:(nt + fc + 1) * 128, :], ko.ap[:, 1, :, :].rearrange("p o c -> p (o c)"))],
                      [ko], [Kf_t[(l, s_)]], ko)
            P.barrier()
            P.pbot = pm_f

        def build_bc(l, which, streams, persist_gate, half_gate):
            ks, kscale, kg = 3 * which, 3 * which + 1, 3 * which + 2
            Cc = {}
            for s_ in streams:
                Cc[s_] = (P.palloc if persist_gate else P.talloc)(f"Cc{s_}", [D], F32)
                P.dma(sp, [(Cc[s_].ap, modv[l, s_:s_ + 1, kg * D:(kg + 1) * D].to_broadcast([128, D]))], [modv_t[l]], [Cc[s_]], Cc[s_])
                if half_gate:
                    P.op(dve, lambda: V.tensor_scalar(out=Cc[s_].ap, in0=Cc[s_].ap, scalar1=0.5, scalar2=None, op0=ALU.mult),
                         [Cc[s_]], [Cc[s_]])
            return Cc

        def build_cols(l, which, streams):
            ks, kscale = 3 * which, 3 * which + 1
            gc = P.talloc("gcl", [DC], F32)
            P.dma(sp, [(gc.ap, norm_g[l, which, :].rearrange("(dc p) -> p dc", p=128))], [dmisc], [gc], gc)
            gcol, scol = {}, {}
            for s_ in streams:
                sct = P.talloc(f"sctc{s_}", [DC], F32)
                gcol[s_] = P.talloc(f"gcol{s_}", [DC], F32)
                scol[s_] = P.talloc(f"scol{s_}", [DC], F32)
                P.dma(sp, [(sct.ap, modv[l, s_, kscale * D:(kscale + 1) * D].rearrange("(dc p) -> p dc", p=128))], [modv_t[l]], [sct], sct)
                P.dma(sp, [(scol[s_].ap, modv[l, s_, ks * D:(ks + 1) * D].rearrange("(dc p) -> p dc", p=128))], [modv_t[l]], [scol[s_]], scol[s_])
                P.op(dve, lambda: V.scalar_tensor_tensor(out=gcol[s_].ap, in0=sct.ap, scalar=1.0, in1=gc.ap, op0=ALU.add, op1=ALU.mult),
                     [sct, gc], [gcol[s_]])
            return gcol, scol

        def norm_preload(srcs, njs):
            NH = 4
            hbs = [P.talloc(f"nh{i}", [D], F32) for i in range(NH)]
            n = min(NH, njs)
            for j in range(n):
                P.dma(sp, [(hbs[j].ap, srcs[j].ap)], [srcs[j]], [hbs[j]], hbs[j])
            return {"hbs": hbs, "n": n}

        def phase_norm(tiles, srcs, gcol, scol, yT, yblk, blocks, pre=None):
            NH, NY = 4, 8
            hbs = pre["hbs"] if pre else [P.talloc(f"nh{i}", [D], F32) for i in range(NH)]
            junk = P.talloc("njunk", [D], BF16)
            ybs = [P.talloc(f"nyb{i}", [D], BF16) for i in range(NY)]
            sss = [P.talloc(f"nss{i}", [1], F32) for i in range(4)]
            rss = [P.talloc(f"nrs{i}", [1], F32) for i in range(4)]
            rds = [P.talloc(f"nrd{i}", [1], F32) for i in range(4)]
            cnt = [0]
            ybof = {}

            def stage_a(j):
                i = cnt[0]
                cnt[0] += 1
                hb = hbs[i % NH]
                if not (pre and i < pre["n"] and i == j):
                    P.dma(sp, [(hb.ap, srcs[j].ap)], [srcs[j]], [hb], hb)
                ss, rs, rd, yb = sss[i % 4], rss[i % 4], rds[i % 4], ybs[i % NY]
                ybof[j] = yb
                P.op(act, lambda: S_.activation(out=junk.ap, in_=hb.ap, func=AF.Square, accum_out=ss.ap), [hb], [junk, ss])
                P.op(act, lambda: S_.activation(out=rs.ap, in_=ss.ap, func=AF.Sqrt, scale=1.0 / D, bias=EPS), [ss], [rs])
                P.op(dve, lambda: V.reciprocal(out=rd.ap, in_=rs.ap), [rs], [rd])
                P.op(dve, lambda: V.tensor_scalar(out=yb.ap, in0=hb.ap, scalar1=rd.ap, scalar2=None, op0=ALU.mult), [hb, rd], [yb])

            def stage_b(b):
                t0, n, s_, tl = blocks[b]
                js = list(range(t0 // 128, (t0 + n) // 128))
                for dc in range(DC):
                    bank = P.bank[(b % 2) * 4 + (dc // 2) % 4]
                    bview = bank.ap.bitcast(BF16)
                    c0 = (dc % 2) * 512
                    for qi_, j in enumerate(js):
                        yb = ybof[j]
                        P.op(pe, lambda: PE_.transpose(out=bview[:, c0 + qi_ * 128:c0 + (qi_ + 1) * 128], in_=yb.ap[:, dc * 128:(dc + 1) * 128],
                                                       identity=identb.ap), [yb, identb], [bank], inc=(qi_ == len(js) - 1))
                    if dc % 3 == 0:
                        P.op(act, lambda: S_.activation(out=yT.ap[:, dc, t0:t0 + n], in_=bview[:, c0:c0 + n], func=AF.Identity,
                                                        scale=gcol[s_].ap[:, dc:dc + 1], bias=scol[s_].ap[:, dc:dc + 1]),
                             [bank, gcol[s_], scol[s_]], [yblk[b]])
                    else:
                        P.op(dve, lambda: V.tensor_scalar(out=yT.ap[:, dc, t0:t0 + n], in0=bview[:, c0:c0 + n], scalar1=gcol[s_].ap[:, dc:dc + 1],
                                                          scalar2=scol[s_].ap[:, dc:dc + 1], op0=ALU.mult, op1=ALU.add),
                             [bank, gcol[s_], scol[s_]], [yblk[b]])

            for b, (t0, n, s_, tl) in enumerate(blocks):
                for j in range(t0 // 128, (t0 + n) // 128):
                    stage_a(j)
                if b >= 1:
                    stage_b(b - 1)
            stage_b(len(blocks) - 1)

        def sublayer_ffn(l, w):
            last = (l == DEPTH - 1)
            final = last and w == 1
            streams = [0] if final else [0, 1]
            which = 0 if w == 0 else 2
            tiles = [(j, 0) for j in range(LT)] + ([] if final else [(j, 1) for j in range(LT, TT)])
            blocks = blocks_of(streams)
            Teff = sum(b[1] for b in blocks)
            srcs = Xd if (l == 0 and w == 0) else Hd
            pm = P.pbot
            yT = P.palloc("yT", [DC, Teff], BF16)
            yblk = [P.reg(Tile(f"yblk{i}", yT.ap)) for i in range(len(blocks))]
            pre = norm_preload(srcs, len(tiles))
            Cc = build_bc(l, which, streams, True, True)
            gcol, scol = build_cols(l, which, streams)
            phase_norm(tiles, srcs, gcol, scol, yT, yblk, blocks, pre=pre)
            P.barrier()
            if stop == "ffnnorm":
                P.pbot = pm
                return
            groups = [list(a) for a in np.array_split(np.arange(FC), cfg.NG)]
            maxg = max(len(g) for g in groups)
            aT = P.talloc("aT", [maxg, Teff], BF16)
            aTt = [P.reg(Tile(f"aT{i}", aT.ap)) for i in range(len(blocks))]
            Wd = [P.talloc(f"Wd{k}", [D], BF16) for k in range(maxg)]
            wgs = [P.talloc(f"wg{i}", [DC, 128], BF16) for i in range(3)]
            wus = [P.talloc(f"wu{i}", [DC, 128], BF16) for i in range(3)]
            sgs = [P.talloc(f"sg{i}", [512], F32) for i in range(2)]
            tts = [P.talloc(f"tt{i}", [512], F32) for i in range(2)]
            NHB = 4
            hbs = [P.talloc(f"fh{i}", [D], F32) for i in range(NHB)]
            wgsrc = w_gate[l, w].rearrange("(dc p) f -> p dc f", p=128)
            wusrc = w_up[l, w].rearrange("(dc p) f -> p dc f", p=128)
            cntw = cntb = cnto = 0
            nhalf = (D + 511) // 512
            for g, fcs in enumerate(groups):
                for k, fc in enumerate(fcs):
                    fc = int(fc)
                    wg, wu = wgs[cntw % 3], wus[cntw % 3]
                    cntw += 1
                    P.dma(pool, [(wg.ap, wgsrc[:, :, fc * 128:(fc + 1) * 128])], [dmisc], [wg], wg)
                    P.dma(pool, [(wu.ap, wusrc[:, :, fc * 128:(fc + 1) * 128])], [dmisc], [wu], wu)
                    P.dma(pool, [(Wd[k].ap, w_down[l, w][fc * 128:(fc + 1) * 128, :])], [dmisc], [Wd[k]], Wd[k])
                    for b, (t0, n, s_, tl) in enumerate(blocks):
                        lt0 = t0 if not final else t0
                        bg, bu = P.bank[2 * (cntb % 2)], P.bank[2 * (cntb % 2) + 1]
                        for dc in range(DC):
                            P.op(pe, lambda: PE_.matmul(bg.ap[:, 0:n], lhsT=wg.ap[:, dc, :], rhs=yT.ap[:, dc, lt0:lt0 + n],
                                                        start=(dc == 0), stop=(dc == DC - 1)), [wg, yblk[b]], [bg], inc=(dc == DC - 1))
                        for dc in range(DC):
                            P.op(pe, lambda: PE_.matmul(bu.ap[:, 0:n], lhsT=wu.ap[:, dc, :], rhs=yT.ap[:, dc, lt0:lt0 + n],
                                                        start=(dc == 0), stop=(dc == DC - 1)), [wu, yblk[b]], [bu], inc=(dc == DC - 1))
                        sg = sgs[cntb % 2]
                        cntb += 1
                        P.op(act, lambda: S_.activation(out=sg.ap[:, 0:n], in_=bg.ap[:, 0:n], func=AF.Silu), [bg], [sg])
                        P.op(dve, lambda: V.tensor_tensor(out=aT.ap[:, k, lt0:lt0 + n], in0=sg.ap[:, 0:n], in1=bu.ap[:, 0:n], op=ALU.mult),
                             [sg, bu], [aTt[b]])
                nk = len(fcs)
                loaded = 0

                def ensure_loaded(upto):
                    nonlocal loaded
                    while loaded <= min(upto, len(tiles) - 1):
                        jj, _ = tiles[loaded]
                        rdt = srcs[jj] if g == 0 else Hd[jj]
                        hbx = hbs[loaded % NHB]
                        P.dma(sp, [(hbx.ap, rdt.ap)], [rdt], [hbx], hbx)
                        loaded += 1

                for i, (j, s_) in enumerate(tiles):
                    ensure_loaded(i + 2)
                    hb = hbs[i % NHB]
                    bi = blk_index(blocks, j * 128)
                    for hf in range(nhalf):
                        n = min(512, D - hf * 512)
                        bo = P.bank[4 + (cnto % 4)]
                        tt = tts[cnto % 2]
                        cnto += 1
                        for k in range(nk):
                            P.op(pe, lambda: PE_.matmul(bo.ap[:, 0:n], lhsT=aT.ap[:, k, j * 128:(j + 1) * 128],
                                                        rhs=Wd[k].ap[:, hf * 512:hf * 512 + n], start=(k == 0), stop=(k == nk - 1)),
                                 [aTt[bi], Wd[k]], [bo], inc=(k == nk - 1))
                        P.op(dve, lambda: V.tensor_tensor(out=tt.ap[:, 0:n], in0=bo.ap[:, 0:n], in1=Cc[s_].ap[:, hf * 512:hf * 512 + n], op=ALU.mult),
                             [bo, Cc[s_]], [tt])
                        P.op(pool, lambda: G_.tensor_tensor(out=hb.ap[:, hf * 512:hf * 512 + n], in0=hb.ap[:, hf * 512:hf * 512 + n],
                                                            in1=tt.ap[:, 0:n], op=ALU.add), [hb, tt], [hb])
                    wr = Od[j] if (final and g == len(groups) - 1) else Hd[j]
                    P.dma(sp, [(wr.ap, hb.ap)], [hb], [wr], hb)
            P.barrier()
            P.pbot = pm

        def sublayer_mixer(l):
            last = (l == DEPTH - 1)
            streams_all = [0, 1]
            mstreams = [0] if last else [0, 1]
            tiles = [(j, 0) for j in range(LT)] + [(j, 1) for j in range(LT, TT)]
            blocks = blocks_of(streams_all)
            mblocks = blocks_of(mstreams)
            pm = P.pbot
            uT_mark = None
            ks_g = 5
            Cc = {}
            for s_ in mstreams:
                Cc[s_] = P.palloc(f"mCc{s_}", [D], F32)
                P.dma(sp, [(Cc[s_].ap, modv[l, s_:s_ + 1, ks_g * D:(ks_g + 1) * D].to_broadcast([128, D]))], [modv_t[l]], [Cc[s_]], Cc[s_])
            mix_pool = P.palloc("mix_pool", [2, T], BF16)
            zc = {s_: P.palloc(f"zc{s_}", [6, (L if s_ == 0 else C)], BF16) for s_ in mstreams}
            qT = P.palloc("qT", [4, T], BF16)
            kdup = P.palloc("kdup", [2, T], BF16)
            vtok = P.palloc("vtok", [TT, 2, 128], BF16)
            mark_u = P.pbot
            uT = P.palloc("uT", [DC, T], BF16)
            ublk = [P.reg(Tile(f"ublk{i}", uT.ap)) for i in range(len(blocks))]
            pre = norm_preload(Hd, len(tiles))
            gcol, scol = build_cols(l, 1, streams_all)
            phase_norm(tiles, Hd, gcol, scol, uT, ublk, blocks, pre=pre)
            P.barrier()
            if stop == "m1":
                P.pbot = pm
                return

            wsrc = w_in[l].rearrange("(dc p) f -> p dc f", p=128)
            cntw = [0]
            cntb = [0]

            def load_w(slots, col0, ncols=128):
                wt = slots[cntw[0] % len(slots)]
                cntw[0] += 1
                P.dma(pool, [(wt.ap[:, :, 0:ncols], wsrc[:, :, col0:col0 + ncols])], [dmisc], [wt], wt)
                return wt

            def proj(wt, b, bank, ncols=128):
                t0, n, s_, tl = blocks[b]
                for dc in range(DC):
                    P.op(pe, lambda: PE_.matmul(bank.ap[0:ncols, 0:n], lhsT=wt.ap[:, dc, 0:ncols], rhs=uT.ap[:, dc, t0:t0 + n],
                                                start=(dc == 0), stop=(dc == DC - 1)), [wt, ublk[b]], [bank], inc=(dc == DC - 1))

            wis = [P.talloc(f"wi{i}", [DC, 128], BF16) for i in range(3)]
            pbuf = {s_: P.talloc(f"pbuf{s_}", [2, (L if s_ == 0 else C) + 16], F32) for s_ in mstreams}
            for s_ in mstreams:
                Lx = L if s_ == 0 else C
                P.op(pool, lambda: G_.memset(pbuf[s_].ap[:, :, 0:8], 0.0), [], [pbuf[s_]])
                P.op(pool, lambda: G_.memset(pbuf[s_].ap[:, :, 8 + Lx:16 + Lx], 0.0), [], [pbuf[s_]])
            for ci in range(2):
                wt = load_w(wis, ci * 128)
                for b, (t0, n, s_, tl) in enumerate(blocks):
                    if s_ not in mstreams:
                        continue
                    bank = P.bank[cntb[0] % 2]
                    cntb[0] += 1
                    proj(wt, b, bank)
                    P.op(act, lambda: S_.copy(out=pbuf[s_].ap[:, ci, 8 + tl:8 + tl + n], in_=bank.ap[:, 0:n]), [bank], [pbuf[s_]])
            Wbd = [P.talloc(f"Wbd{ci}", [128], BF16) for ci in range(2)]
            psc = P.talloc("psc", [2], F32)
            P.dma(sp, [(psc.ap, pool_scale[l, :].rearrange("(k p) -> p k", p=128))], [dmisc], [psc], psc)
            for ci in range(2):
                P.op(pool, lambda: G_.memset(Wbd[ci].ap, 0.0), [], [Wbd[ci]])
                P.dma(pool, [(Wbd[ci].ap[0:64, 0:64], pool_w[l, 2 * ci]), (Wbd[ci].ap[64:128, 64:128], pool_w[l, 2 * ci + 1])],
                      [dmisc], [Wbd[ci]], Wbd[ci])
            for s_ in mstreams:
                Lx = L if s_ == 0 else C
                g0 = 0 if s_ == 0 else L
                tA = P.talloc(f"ptA{s_}", [Lx + 16], F32)
                tB = P.talloc(f"ptB{s_}", [Lx + 16], F32)
                Sf = P.talloc(f"pSf{s_}", [Lx], F32)
                pooled = P.talloc(f"pooled{s_}", [2, Lx], BF16)
                etmp = P.talloc(f"petmp{s_}", [8], F32)
                for ci in range(2):
                    for hh in range(2):
                        w_ = POOL_WINDOWS[2 * ci + hh]
                        pr = slice(hh * 64, (hh + 1) * 64)
                        off = 8 - w_ // 2
                        length = Lx + w_ - 1
                        cur = pbuf[s_].ap[pr, ci, off:off + length]
                        cur_t = pbuf[s_]
                        m = 1
                        bi_ = 0
                        while m < w_:
                            newlen = length - m
                            fin = (2 * m == w_)
                            dst_t = Sf if fin else (tA, tB)[bi_]
                            dst = dst_t.ap[pr, 0:newlen]
                            P.op(dve, lambda: V.tensor_tensor(out=dst, in0=cur[:, 0:newlen], in1=cur[:, m:m + newlen], op=ALU.add),
                                 [cur_t], [dst_t])
                            cur, cur_t, length = dst, dst_t, newlen
                            m *= 2
                            bi_ ^= 1
                    P.op(dve, lambda: V.scalar_tensor_tensor(out=pooled.ap[:, ci, :], in0=Sf.ap, scalar=invw.ap[:, ci:ci + 1],
                                                             in1=pbuf[s_].ap[:, ci, 8:8 + Lx], op0=ALU.mult, op1=ALU.subtract),
                         [Sf, invw, pbuf[s_]], [pooled])
                    for (c0, e0) in ((0, 0), (Lx - 8, 8)):
                        P.op(dve, lambda: V.tensor_tensor(out=etmp.ap, in0=Sf.ap[:, c0:c0 + 8], in1=invc.ap[:, ci, e0:e0 + 8], op=ALU.mult),
                             [Sf, invc], [etmp])
                        P.op(dve, lambda: V.tensor_tensor(out=pooled.ap[:, ci, c0:c0 + 8], in0=etmp.ap,
                                                          in1=pbuf[s_].ap[:, ci, 8 + c0:16 + c0], op=ALU.subtract), [etmp, pbuf[s_]], [pooled])
                    t0 = 0
                    while t0 < Lx:
                        n = min(512, Lx - t0)
                        bank = P.bank[2 + cntb[0] % 2]
                        cntb[0] += 1
                        P.op(pe, lambda: PE_.matmul(bank.ap[:, 0:n], lhsT=Wbd[ci].ap, rhs=pooled.ap[:, ci, t0:t0 + n], start=True, stop=True),
                             [Wbd[ci], pooled], [bank])
                        P.op(dve, lambda: V.tensor_scalar(out=mix_pool.ap[:, ci, g0 + t0:g0 + t0 + n], in0=bank.ap[:, 0:n],
                                                          scalar1=psc.ap[:, ci:ci + 1], scalar2=None, op0=ALU.mult), [bank, psc], [mix_pool])
                        t0 += n
            P.barrier()
            if stop == "m2a":
                P.pbot = pm
                return

            wis = [P.talloc(f"wi{i}", [DC, 128], BF16) for i in range(3)]
            hp = {s_: P.talloc(f"hp{s_}", [6, (L if s_ == 0 else C) + 2], BF16) for s_ in mstreams}
            cw = P.talloc("cw", [6, 3], F32)
            cbv = P.talloc("cbv", [6], F32)
            P.dma(sp, [(cw.ap[:, :, k], hconv_w[l, k, :].rearrange("(ch p) -> p ch", p=128)) for k in range(3)], [dmisc], [cw], cw)
            P.dma(sp, [(cbv.ap, hconv_b[l, :].rearrange("(ch p) -> p ch", p=128))], [dmisc], [cbv], cbv)
            for s_ in mstreams:
                Lx = L if s_ == 0 else C
                P.op(pool, lambda: G_.memset(hp[s_].ap[:, :, 0:1], 0.0), [], [hp[s_]])
                P.op(pool, lambda: G_.memset(hp[s_].ap[:, :, Lx + 1:Lx + 2], 0.0), [], [hp[s_]])
            hpt = {s_: [P.reg(Tile(f"hpt{s_}_{ch}", hp[s_].ap)) for ch in range(6)] for s_ in mstreams}
            CH = 1024
            tas = [P.talloc(f"cta{i}", [CH], F32) for i in range(2)]
            tbs = [P.talloc(f"ctb{i}", [CH], F32) for i in range(2)]
            cc_ = 0
            for ch in range(6):
                wt = load_w(wis, HY_OFF + ch * 128)
                for b, (t0, n, s_, tl) in enumerate(blocks):
                    if s_ not in mstreams:
                        continue
                    bank = P.bank[cntb[0] % 4]
                    cntb[0] += 1
                    proj(wt, b, bank)
                    P.op(act, lambda: S_.copy(out=hp[s_].ap[:, ch, 1 + tl:1 + tl + n], in_=bank.ap[:, 0:n]), [bank], [hpt[s_][ch]])
                for s_ in mstreams:
                    Lx = L if s_ == 0 else C
                    t0 = 0
                    while t0 < Lx:
                        n = min(CH, Lx - t0)
                        ta, tb = tas[cc_ % 2], tbs[cc_ % 2]
                        cc_ += 1
                        P.op(dve, lambda: V.tensor_scalar(out=ta.ap[:, 0:n], in0=hp[s_].ap[:, ch, 1 + t0:1 + t0 + n], scalar1=cw.ap[:, ch, 1:2],
                                                          scalar2=cbv.ap[:, ch:ch + 1], op0=ALU.mult, op1=ALU.add), [hpt[s_][ch], cw, cbv], [ta])
                        P.op(dve, lambda: V.scalar_tensor_tensor(out=tb.ap[:, 0:n], in0=hp[s_].ap[:, ch, t0:t0 + n], scalar=cw.ap[:, ch, 0:1],
                                                                 in1=ta.ap[:, 0:n], op0=ALU.mult, op1=ALU.add), [hpt[s_][ch], cw, ta], [tb])
                        P.op(dve, lambda: V.scalar_tensor_tensor(out=zc[s_].ap[:, ch, t0:t0 + n], in0=hp[s_].ap[:, ch, 2 + t0:2 + t0 + n],
                                                                 scalar=cw.ap[:, ch, 2:3], in1=tb.ap[:, 0:n], op0=ALU.mult, op1=ALU.add),
                             [hpt[s_][ch], cw, tb], [zc[s_]])
                        t0 += n
            P.barrier()
            if stop == "m2b":
                P.pbot = pm
                return

            P.phase("m2c")
            wis = [P.talloc(f"wi{i}", [DC, 128], BF16) for i in range(3)]
            rcos = P.talloc("rcos", [L], F32)
            rsin = P.talloc("rsin", [L], F32)
            P.dma(sp, [(rcos.ap, cst["ropecos"])], [dmisc], [rcos], rcos)
            P.dma(sp, [(rsin.ap, cst["ropesin"])], [dmisc], [rsin], rsin)
            gq = P.talloc("gq", [1], F32)
            gk = P.talloc("gk", [1], F32)
            P.dma(sp, [(gq.ap[0:64, :], q_norm_g[l, :].rearrange("(p o) -> p o", o=1)),
                       (gq.ap[64:128, :], q_norm_g[l, :].rearrange("(p o) -> p o", o=1))], [dmisc], [gq], gq)
            P.dma(sp, [(gk.ap[0:64, :], k_norm_g[l, :].rearrange("(p o) -> p o", o=1)),
                       (gk.ap[64:128, :], k_norm_g[l, :].rearrange("(p o) -> p o", o=1))], [dmisc], [gk], gk)
            NR = 3
            NX = 5
            sqs = [P.talloc(f"qsq{i}", [512], BF16) for i in range(NR)]
            xss = [P.talloc(f"qxs{i}", [512], F32) for i in range(NX)]
            rst = [P.talloc(f"qrs{i}", [512], F32) for i in range(NR)]
            rdt = [P.talloc(f"qrd{i}", [512], F32) for i in range(NX)]
            xns = [P.talloc(f"qxn{i}", [512], BF16) for i in range(NX)]
            ats = [P.talloc(f"qat{i}", [512], F32) for i in range(NR)]
            bts = [P.talloc(f"qbt{i}", [512], F32) for i in range(NR)]
            work = []
            for qi in range(4):
                work.append(("q", qi))
            for kh in range(2):
                work.append(("k", kh))
            units = []
            for kind, idx in work:
                for b, (t0, n, s_, tl) in enumerate(blocks):
                    if kind == "q" and s_ not in mstreams:
                        continue
                    units.append((kind, idx, b))
            wcache = {}

            def get_w(kind, idx):
                if (kind, idx) in wcache:
                    return wcache[(kind, idx)]
                if kind == "q":
                    wt = load_w(wis, Q_OFF + idx * 128)
                else:
                    wt = wis[cntw[0] % 3]
                    cntw[0] += 1
                    P.dma(pool, [(wt.ap[:, :, 0:64], wsrc[:, :, K_OFF + idx * 64:K_OFF + (idx + 1) * 64]),
                                 (wt.ap[:, :, 64:128], wsrc[:, :, K_OFF + idx * 64:K_OFF + (idx + 1) * 64])], [dmisc], [wt], wt)
                wcache[(kind, idx)] = wt
                return wt

            def dest_of(kind, idx, a, b_):
                return (qT.ap[:, idx, a:b_], qT, gq) if kind == "q" else (kdup.ap[:, idx, a:b_], kdup, gk)

            def qk_p(u):
                kind, idx, b = units[u]
                proj(get_w(kind, idx), b, P.bank[u % 3])

            def qk_a(u):
                kind, idx, b = units[u]
                t0, n, s_, tl = blocks[b]
                r_ = u % NR
                bA, bB = P.bank[u % 3], P.bank[3 + u % 2]
                sq, xs, rs, rd = sqs[r_], xss[u % NX], rst[r_], rdt[u % NX]
                P.op(act, lambda: S_.copy(out=xs.ap[:, 0:n], in_=bA.ap[:, 0:n]), [bA], [xs])
                P.op(act, lambda: S_.activation(out=sq.ap[:, 0:n], in_=xs.ap[:, 0:n], func=AF.Square), [xs], [sq])
                P.op(pe, lambda: PE_.matmul(bB.ap[:, 0:n], lhsT=blockonesb.ap, rhs=sq.ap[:, 0:n], start=True, stop=True), [blockonesb, sq], [bB])
                P.op(act, lambda: S_.activation(out=rs.ap[:, 0:n], in_=bB.ap[:, 0:n], func=AF.Ln, scale=1.0 / 64, bias=epsc.ap), [bB, epsc], [rs])
                P.op(act, lambda: S_.activation(out=rd.ap[:, 0:n], in_=rs.ap[:, 0:n], func=AF.Exp, scale=-0.5), [rs], [rd])

            def qk_a2(u):
                kind, idx, b = units[u]
                t0, n, s_, tl = blocks[b]
                xs, rd, xn = xss[u % NX], rdt[u % NX], xns[u % NX]
                dst, dst_t, gvec = dest_of(kind, idx, t0, t0 + n)
                if s_ == 0:
                    P.op(dve, lambda: V.scalar_tensor_tensor(out=xn.ap[:, 0:n], in0=xs.ap[:, 0:n], scalar=gvec.ap, in1=rd.ap[:, 0:n],
                                                             op0=ALU.mult, op1=ALU.mult), [xs, gvec, rd], [xn])
                else:
                    P.op(dve, lambda: V.scalar_tensor_tensor(out=dst, in0=xs.ap[:, 0:n], scalar=gvec.ap, in1=rd.ap[:, 0:n],
                                                             op0=ALU.mult, op1=ALU.mult), [xs, gvec, rd], [dst_t])

            def qk_b(u):
                kind, idx, b = units[u]
                t0, n, s_, tl = blocks[b]
                if s_ != 0:
                    return
                r_ = u % NR
                bC = P.bank[5 + u % 2]
                xn, at, bt = xns[u % NX], ats[r_], bts[r_]
                dst, dst_t, gvec = dest_of(kind, idx, t0, t0 + n)
                P.op(pe, lambda: PE_.matmul(bC.ap[:, 0:n], lhsT=rotmb.ap, rhs=xn.ap[:, 0:n], start=True, stop=True), [rotmb, xn], [bC])
                P.op(pool, lambda: G_.tensor_tensor(out=at.ap[:, 0:n], in0=xn.ap[:, 0:n], in1=rcos.ap[:, tl:tl + n], op=ALU.mult), [xn, rcos], [at])
                P.op(dve, lambda: V.tensor_tensor(out=bt.ap[:, 0:n], in0=bC.ap[:, 0:n], in1=rsin.ap[:, tl:tl + n], op=ALU.mult), [bC, rsin], [bt])
                P.op(dve, lambda: V.tensor_tensor(out=dst, in0=at.ap[:, 0:n], in1=bt.ap[:, 0:n], op=ALU.add), [at, bt], [dst_t])

            nu = len(units)
            for u in range(nu + 4):
                if u < nu:
                    qk_p(u)
                if 0 <= u - 1 < nu:
                    qk_a(u - 1)
                if 0 <= u - 2 < nu:
                    qk_a2(u - 2)
                if 0 <= u - 4 < nu:
                    qk_b(u - 4)
            wv = load_w(wis, V_OFF)
            P.op(pool, lambda: G_.memset(vtok.ap[:, :, :, 64:128], 1.0), [], [vtok])
            for i, (j, s_) in enumerate(tiles):
                bank = P.bank[(i % 2) * 4 + 3]
                bi = blk_index(blocks, j * 128)
                for dc in range(DC):
                    P.op(pe, lambda: PE_.matmul(bank.ap[:, 0:128], lhsT=uT.ap[:, dc, j * 128:(j + 1) * 128], rhs=wv.ap[:, dc, :],
                                                start=(dc == 0), stop=(dc == DC - 1)), [ublk[bi], wv], [bank], inc=(dc == DC - 1))
                P.op(dve, lambda: V.tensor_copy(out=vtok.ap[:, j, :, 0:64], in_=bank.ap[:, 0:128].rearrange("p (h d) -> p h d", h=2)), [bank], [vtok])
            P.barrier()
            if stop == "m2c":
                P.pbot = pm
                return
            P.pbot = mark_u
            mix_att = P.palloc("mix_att", [4, T], BF16)
            mix_hy = P.palloc("mix_hy", [2, T], BF16)

            LA = 4
            pts = [P.talloc(f"pt{i}", [512], BF16) for i in range(6)]
            mod_ems = mod_emitters(l + 1, lambda k: P.bank[5]) if l + 1 < DEPTH else []
            recs = [P.talloc(f"rec{i}", [512], F32) for i in range(2)]
            cnts = 0
            cnto = 0
            for qs in mstreams:
                keytiles = list(range(TT)) if qs == 0 else list(range(LT, TT))
                qblocks = [b for b in blocks if b[2] == qs]
                for h in range(8):
                    kh, qi = h // 4, h // 2
                    pr = slice((h % 2) * 64, (h % 2) * 64 + 64)
                    for (t0, n, s_, tl) in qblocks:
                        bo = P.bank[6 + cnto % 2]
                        rec = recs[cnto % 2]
                        cnto += 1
                        nk = len(keytiles)
                        slots = {}

                        def emit_qk(ii):
                            nonlocal cnts
                            i_ = keytiles[ii]
                            bs, pt = P.bank[cnts % 5], pts[cnts % 6]
                            cnts += 1
                            slots[ii] = pt
                            P.op(pe, lambda: PE_.matmul(bs.ap[:, 0:n], lhsT=kdup.ap[pr, kh, i_ * 128:(i_ + 1) * 128], rhs=qT.ap[pr, qi, t0:t0 + n],
                                                        start=True, stop=True), [kdup, qT], [bs])
                            P.op(act, lambda: S_.activation(out=pt.ap[:, 0:n], in_=bs.ap[:, 0:n], func=AF.Exp, scale=ATTN_SCALE), [bs], [pt])

                        def emit_pv(ii):
                            i_ = keytiles[ii]
                            pt = slots.pop(ii)
                            P.op(pe, lambda: PE_.matmul(bo.ap[:, 0:n], lhsT=vtok.ap[:, i_, kh, :], rhs=pt.ap[:, 0:n],
                                                        start=(ii == 0), stop=(ii == nk - 1)), [vtok, pt], [bo], inc=(ii == nk - 1))

                        for ii in range(nk + LA):
                            if ii < nk:
                                emit_qk(ii)
                            if ii >= LA:
                                emit_pv(ii - LA)
                        P.op(dve, lambda: V.reciprocal(out=rec.ap[64:128, 0:n], in_=bo.ap[64:128, 0:n]), [bo], [rec])
                        P.op(dve, lambda: V.tensor_tensor(out=mix_att.ap[pr, qi, t0:t0 + n], in0=bo.ap[0:64, 0:n], in1=rec.ap[64:128, 0:n], op=ALU.mult),
                             [bo, rec], [mix_att])
                        if mod_ems:
                            mod_ems.pop(0)()
            while mod_ems:
                mod_ems.pop(0)()
            P.barrier()
            if stop == "m3":
                P.pbot = pm
                return

            hbias = P.palloc("hbias", [2, 2], F32)
            P.dma(sp, [(hbias.ap[:, o, :], hy_bias[l, o, :].rearrange("(cc p) -> p cc", p=128)) for o in range(2)], [dmisc], [hbias], hbias)
            for s_ in mstreams:
                Lx = L if s_ == 0 else C
                g0 = 0 if s_ == 0 else L
                tag = "L" if s_ == 0 else "C"
                nt = Lx // 128
                Fh, Gh, kf = cst["Fh_" + tag], cst["Gh_" + tag], Kf[(l, s_)]
                ztok = P.talloc(f"ztok{s_}", [nt, 256], BF16)
                z2f = P.reg(Tile(f"z2f{s_}", mix_hy.ap[:, :, g0:g0 + Lx]))
                Y = P.talloc(f"Y{s_}", [2 * nt, 256], BF16)
                Fsl = [[P.talloc(f"hF{s_}{i}_{k}", [nt * 128], BF16) for k in range(2)] for i in range(2)]
                Ksl = [[P.talloc(f"hK{s_}{i}_{k}", [256], F32) for k in range(2)] for i in range(2)]
                GR = 4
                NGS = 4
                Gsl = [P.talloc(f"hG{s_}{i}", [GR, 512], BF16) for i in range(NGS)]
                tq = [[P.talloc(f"hq{s_}{i}_{k}", [256], F32) for k in range(4)] for i in range(2)]
                tfs = [P.talloc(f"htf{s_}{i}", [512], F32) for i in range(2)]
                sblocks = [b for b in blocks if b[2] == s_]
                cx = 0
                cg = 0
                ce = 0
                for o in range(2):
                    zin = (lambda cc: zc[s_].ap[:, cc, :]) if o == 0 else (lambda cc: z2f.ap[:, cc, :])
                    zin_t = zc[s_] if o == 0 else z2f
                    gate = lambda cc: zc[s_].ap[:, 2 + 2 * o + cc, :]
                    for tc in range(nt):
                        bank = P.bank[cx % 2]
                        bview = bank.ap.bitcast(BF16)
                        for cc in range(2):
                            P.op(pe, lambda: PE_.transpose(out=bview[:, cc * 128:(cc + 1) * 128], in_=zin(cc)[:, tc * 128:(tc + 1) * 128],
                                                           identity=identb.ap), [zin_t, identb], [bank], inc=(cc == 1))
                        if cx % 2 == 0:
                            P.op(dve, lambda: V.tensor_copy(out=ztok.ap[:, tc, :], in_=bview[:, 0:256]), [bank], [ztok])
                        else:
                            P.op(act, lambda: S_.copy(out=ztok.ap[:, tc, :], in_=bview[:, 0:256]), [bank], [ztok])
                        cx += 1
                    for fc in range(nt):
                        Fc_, Fs_ = Fsl[fc % 2]
                        Kr, Ki = Ksl[fc % 2]
                        P.dma(sp, [(Fc_.ap, Fh[fc])], [dmisc], [Fc_], Fc_)
                        P.dma(sp, [(Fs_.ap, Fh[nt + fc])], [dmisc], [Fs_], Fs_)
                        P.dma(sp, [(Kr.ap, kf[fc * 128:(fc + 1) * 128, o * 256:(o + 1) * 256])], [Kf_t[(l, s_)]], [Kr], Kr)
                        P.dma(sp, [(Ki.ap, kf[(nt + fc) * 128:(nt + fc + 1) * 128, o * 256:(o + 1) * 256])], [Kf_t[(l, s_)]], [Ki], Ki)
                        bUc, bUs = P.bank[2 + 2 * (fc % 2)], P.bank[3 + 2 * (fc % 2)]
                        for sc in range(nt):
                            P.op(pe, lambda: PE_.matmul(bUc.ap[:, 0:256], lhsT=Fc_.ap[:, sc * 128:(sc + 1) * 128], rhs=ztok.ap[:, sc, :],
                                                        start=(sc == 0), stop=(sc == nt - 1)), [Fc_, ztok], [bUc], inc=(sc == nt - 1))
                        for sc in range(nt):
                            P.op(pe, lambda: PE_.matmul(bUs.ap[:, 0:256], lhsT=Fs_.ap[:, sc * 128:(sc + 1) * 128], rhs=ztok.ap[:, sc, :],
                                                        start=(sc == 0), stop=(sc == nt - 1)), [Fs_, ztok], [bUs], inc=(sc == nt - 1))
                        q1, q2, q3, q4 = tq[fc % 2]
                        P.op(dve, lambda: V.tensor_tensor(out=q1.ap, in0=bUc.ap[:, 0:256], in1=Kr.ap, op=ALU.mult), [bUc, Kr], [q1])
                        P.op(dve, lambda: V.tensor_tensor(out=q2.ap, in0=bUs.ap[:, 0:256], in1=Ki.ap, op=ALU.mult), [bUs, Ki], [q2])
                        P.op(pool, lambda: G_.tensor_tensor(out=Y.ap[:, fc, :], in0=q1.ap, in1=q2.ap, op=ALU.subtract), [q1, q2], [Y])
                        P.op(dve, lambda: V.tensor_tensor(out=q3.ap, in0=bUc.ap[:, 0:256], in1=Ki.ap, op=ALU.mult), [bUc, Ki], [q3])
                        P.op(dve, lambda: V.tensor_tensor(out=q4.ap, in0=bUs.ap[:, 0:256], in1=Kr.ap, op=ALU.mult), [bUs, Kr], [q4])
                        P.op(pool, lambda: G_.tensor_tensor(out=Y.ap[:, nt + fc, :], in0=q3.ap, in1=q4.ap, op=ALU.add), [q3, q4], [Y])
                    Gv = Gh.rearrange("(r p) t -> p r t", p=128)
                    for (t0, n, _s, tl) in sblocks:
                        bO = [P.bank[6], P.bank[7]]
                        rc = 0
                        while rc < 2 * nt:
                            ng = min(GR, 2 * nt - rc)
                            gs = Gsl[cg % NGS]
                            cg += 1
                            P.dma(sp, [(gs.ap[:, 0:ng, 0:n], Gv[:, rc:rc + ng, tl:tl + n])], [dmisc], [gs], gs)
                            for r in range(ng):
                                for cc in range(2):
                                    P.op(pe, lambda: PE_.matmul(bO[cc].ap[:, 0:n], lhsT=Y.ap[:, rc + r, cc * 128:(cc + 1) * 128], rhs=gs.ap[:, r, 0:n],
                                                                start=(rc + r == 0), stop=(rc + r == 2 * nt - 1)), [Y, gs], [bO[cc]],
                                         inc=(rc + r == 2 * nt - 1) or (r == ng - 1 and cc == 1))
                            rc += ng
                        for cc in range(2):
                            tf = tfs[ce % 2]
                            ce += 1
                            P.op(dve, lambda: V.scalar_tensor_tensor(out=tf.ap[:, 0:n], in0=zin(cc)[:, tl:tl + n], scalar=hbias.ap[:, o, cc:cc + 1],
                                                                     in1=bO[cc].ap[:, 0:n], op0=ALU.mult, op1=ALU.add), [zin_t, hbias, bO[cc]], [tf])
                            if o == 0:
                                P.op(pool, lambda: G_.tensor_tensor(out=z2f.ap[:, cc, tl:tl + n], in0=tf.ap[:, 0:n], in1=gate(cc)[:, tl:tl + n], op=ALU.mult),
                                     [tf, zc[s_]], [z2f])
                            else:
                                P.op(pool, lambda: G_.tensor_tensor(out=mix_hy.ap[:, cc, g0 + tl:g0 + tl + n], in0=tf.ap[:, 0:n], in1=gate(cc)[:, tl:tl + n],
                                                                    op=ALU.mult), [tf, zc[s_]], [mix_hy, z2f])
                P.barrier()
            if stop == "m4":
                P.pbot = pm
                return

            Wo = P.talloc("Wo", [8, D], BF16)
            P.dma(pool, [(Wo.ap[:, mc, :], w_out[l][mc * 128:(mc + 1) * 128, :]) for mc in range(8)], [dmisc], [Wo], Wo)
            NHB = 4
            hbs = [P.talloc(f"oh{i}", [D], F32) for i in range(NHB)]
            tts = [P.talloc(f"ott{i}", [512], F32) for i in range(2)]
            otiles = [(j, s_) for (j, s_) in tiles if s_ in mstreams]
            mixsrc = lambda mc, a, b_: (mix_pool.ap[:, mc, a:b_] if mc < 2 else (mix_hy.ap[:, mc - 2, a:b_] if mc < 4 else mix_att.ap[:, mc - 4, a:b_]))
            nhalf = (D + 511) // 512
            cnto = 0
            loaded = 0
            for i, (j, s_) in enumerate(otiles):
                while loaded <= min(i + 2, len(otiles) - 1):
                    jj = otiles[loaded][0]
                    P.dma(sp, [(hbs[loaded % NHB].ap, Hd[jj].ap)], [Hd[jj]], [hbs[loaded % NHB]], hbs[loaded % NHB])
                    loaded += 1
                hb = hbs[i % NHB]
                for hf in range(nhalf):
                    n = min(512, D - hf * 512)
                    bo = P.bank[cnto % 4]
                    tt = tts[cnto % 2]
                    cnto += 1
                    for mc in range(8):
                        P.op(pe, lambda: PE_.matmul(bo.ap[:, 0:n], lhsT=mixsrc(mc, j * 128, (j + 1) * 128), rhs=Wo.ap[:, mc, hf * 512:hf * 512 + n],
                                                    start=(mc == 0), stop=(mc == 7)), [mix_pool, mix_hy, mix_att, Wo], [bo], inc=(mc == 7))
                    P.op(dve, lambda: V.tensor_tensor(out=tt.ap[:, 0:n], in0=bo.ap[:, 0:n], in1=Cc[s_].ap[:, hf * 512:hf * 512 + n], op=ALU.mult),
                         [bo, Cc[s_]], [tt])
                    P.op(pool, lambda: G_.tensor_tensor(out=hb.ap[:, hf * 512:hf * 512 + n], in0=hb.ap[:, hf * 512:hf * 512 + n], in1=tt.ap[:, 0:n],
                                                        op=ALU.add), [hb, tt], [hb])
                P.dma(sp, [(Hd[j].ap, hb.ap)], [hb], [Hd[j]], hb)
            P.barrier()
            P.pbot = pm

        stop = getattr(cfg, "stop", None)
        P.mute = getattr(cfg, "mute", None)

        def run_all():
            P.barrier()
            if stop == "init":
                return
            phase_mod()
            if stop == "mod":
                return
            for l in range(DEPTH):
                for s_ in (0, 1):
                    if s_ == 1 and l == DEPTH - 1:
                        continue
                    phase_filters(l, s_)
                    if stop == "filt":
                        return
            if stop == "filtall":
                return
            for l in range(DEPTH):
                sublayer_ffn(l, 0)
                if stop in ("ffn", "ffnnorm"):
                    return
                sublayer_mixer(l)
                if stop in ("mix", "m1", "m2a", "m2b", "m2c", "m3", "m4"):
                    return
                sublayer_ffn(l, 1)

        run_all()
        P.barrier()
    return nc


_WNAMES = ["norm_g", "w_mod", "b_mod", "ffn_w_gate", "ffn_w_up", "ffn_w_down", "w_in", "w_out", "pool_w", "pool_scale",
           "hyena_conv_w", "hyena_conv_b", "hyena_f_w1", "hyena_f_b1", "hyena_f_w2", "hyena_f_b2", "hyena_f_w3",
           "hyena_sin_freq", "hyena_bias", "q_norm_g", "k_norm_g"]


def make_in_maps(cfg, consts, inputs, B):
    f = lambda a: np.ascontiguousarray(np.asarray(a, dtype=np.float32))
    shared = {n: f(inputs[n]) for n in _WNAMES}
    for k, v in consts.items():
        shared["k_" + k] = v
    xs, cs, ctxs = f(inputs["x"]), f(inputs["c"]), f(inputs["ctx"])
    cctx = f(inputs["c_ctx"]).reshape(1, cfg.D)
    maps = []
    for b in range(B):
        m = dict(shared)
        m["x"] = np.ascontiguousarray(xs[b])
        m["ctx"] = np.ascontiguousarray(ctxs[b])
        m["c"] = np.ascontiguousarray(cs[b:b + 1])
        m["c_ctx"] = cctx
        maps.append(m)
    return maps


def kernel(**inputs):
    cfg = Cfg()
    consts = host_constants(cfg)
    nc = build_program(cfg, consts)
    B = 8
    maps = make_in_maps(cfg, consts, inputs, B)
    res = run_bass_kernel_spmd(nc, maps, core_ids=list(range(B)))
    return np.stack([np.asarray(r["out"], dtype=np.float32) for r in res.results], axis=0)
```
